# Optimizing a Trainium2 kernel written in Bass

```python
import math
import jax, jax.numpy as jnp
from jax import lax
import numpy as np

D_MODEL = 2048
BATCH = 8
SEQ = 4096
DEPTH = 2

CTX_LEN = 256
GRID_W = 64
HEAD_DIM = 128
N_GROUPS = 4
GROUP_WIDTH = D_MODEL // N_GROUPS
MIX_WIDTH = N_GROUPS * GROUP_WIDTH
Q_BLOCK = 128
WINDOW = 128
ROPE_THETA = 10000.0
EPS = 1e-6
NEG_INF = -1e30

A_HEADS = GROUP_WIDTH // HEAD_DIM
A_KV_HEADS = A_HEADS // 2
B_HEADS = GROUP_WIDTH // HEAD_DIM
B_V_DIM = HEAD_DIM
B_QK_DIM = B_V_DIM // 2
C_HEADS = GROUP_WIDTH // HEAD_DIM
C_Q_LORA = 448
C_KV_LORA = 128
C_NOPE = 128
C_ROPE = 64
C_V = GROUP_WIDTH // C_HEADS
D_HEADS = GROUP_WIDTH // HEAD_DIM
D_KV_HEADS = 2

IN_SIZES = (A_HEADS * HEAD_DIM, A_KV_HEADS * HEAD_DIM, A_KV_HEADS * HEAD_DIM, GROUP_WIDTH,
            2 * B_HEADS * B_QK_DIM, 2 * B_HEADS * B_QK_DIM, B_HEADS * B_V_DIM, GROUP_WIDTH,
            C_Q_LORA, C_KV_LORA, C_ROPE, GROUP_WIDTH,
            D_HEADS * HEAD_DIM, D_KV_HEADS * HEAD_DIM, D_KV_HEADS * HEAD_DIM, GROUP_WIDTH)
IN_WIDTH = sum(IN_SIZES)

kernel_name = 'hybrid_parallel_heads_dit_ctx_prefix'


def rms_norm(x, g, eps=EPS):
    xf = x.astype(jnp.float32)
    y = xf * lax.rsqrt(jnp.mean(xf * xf, axis=-1, keepdims=True) + eps)
    return (y * g.astype(jnp.float32)).astype(x.dtype)


def axial_rope_tables(rows, rot_dim):
    row = jnp.broadcast_to(jnp.arange(rows)[:, None], (rows, GRID_W)).reshape(-1).astype(jnp.float32)
    col = jnp.broadcast_to(jnp.arange(GRID_W)[None, :], (rows, GRID_W)).reshape(-1).astype(jnp.float32)
    axis_dim = rot_dim // 2
    inv_freq = ROPE_THETA ** (-jnp.arange(0, axis_dim, 2, dtype=jnp.float32) / axis_dim)
    ang_r = row[:, None] * inv_freq[None, :]
    ang_c = col[:, None] * inv_freq[None, :]
    ang = jnp.concatenate([ang_r, ang_r, ang_c, ang_c], axis=-1)
    return jnp.cos(ang), jnp.sin(ang)


def apply_rope(x, rope):
    cos, sin = rope
    x1, x2, x3, x4 = jnp.split(x, 4, axis=-1)
    rot = jnp.concatenate([-x2, x1, -x4, x3], axis=-1)
    return (x * cos[:, None, :] + rot * sin[:, None, :]).astype(x.dtype)


def split_cols(y, sizes):
    cuts = [int(v) for v in np.cumsum(sizes)[:-1]]
    return jnp.split(y, cuts, axis=-1)


def branch_inputs(h, w_in, cq_g, ckv_g, w_uq, w_ukv, dq_g, dk_g, ropes):
    B, T, _ = h.shape
    (a_q, a_k, a_v, a_z, b_q, b_k, b_v, b_z,
     c_cq, c_ckv, c_kr, c_z, d_q, d_k, d_v, d_z) = split_cols(h @ w_in, IN_SIZES)
    a_q = a_q.reshape(B, T, A_HEADS, HEAD_DIM)
    a_k = a_k.reshape(B, T, A_KV_HEADS, HEAD_DIM)
    a_v = a_v.reshape(B, T, A_KV_HEADS, HEAD_DIM)
    b_q = b_q.reshape(B, T, 2 * B_HEADS, B_QK_DIM)
    b_k = b_k.reshape(B, T, 2 * B_HEADS, B_QK_DIM)
    b_v = b_v.reshape(B, T, B_HEADS, B_V_DIM)
    c_qh = (rms_norm(c_cq, cq_g) @ w_uq).reshape(B, T, C_HEADS, C_NOPE + C_ROPE)
    c_q_nope, c_q_rope = c_qh[..., :C_NOPE], c_qh[..., C_NOPE:]
    c_kvh = (rms_norm(c_ckv, ckv_g) @ w_ukv).reshape(B, T, C_HEADS, C_NOPE + C_V)
    c_k_nope, c_v = c_kvh[..., :C_NOPE], c_kvh[..., C_NOPE:]
    c_kr = c_kr.reshape(B, T, 1, C_ROPE)
    d_q = rms_norm(d_q.reshape(B, T, D_HEADS, HEAD_DIM), dq_g)
    d_k = rms_norm(d_k.reshape(B, T, D_KV_HEADS, HEAD_DIM), dk_g)
    d_v = d_v.reshape(B, T, D_KV_HEADS, HEAD_DIM)
    if ropes is not None:
        rope_h, rope_b, rope_c = ropes
        a_q, a_k = apply_rope(a_q, rope_h), apply_rope(a_k, rope_h)
        b_q, b_k = apply_rope(b_q, rope_b), apply_rope(b_k, rope_b)
        c_q_rope, c_kr = apply_rope(c_q_rope, rope_c), apply_rope(c_kr, rope_c)
        d_q, d_k = apply_rope(d_q, rope_h), apply_rope(d_k, rope_h)
    c_q = jnp.concatenate([c_q_nope, c_q_rope], axis=-1)[:, :, :, None, :]
    c_k = jnp.concatenate([c_k_nope, jnp.broadcast_to(c_kr, (B, T, C_HEADS, C_ROPE))], axis=-1)
    return {
        'a_q': a_q.reshape(B, T, A_KV_HEADS, A_HEADS // A_KV_HEADS, HEAD_DIM), 'a_k': a_k, 'a_v': a_v, 'a_z': a_z,
        'b_q': b_q.reshape(B, T, B_HEADS, 2, B_QK_DIM), 'b_k': b_k.reshape(B, T, B_HEADS, 2, B_QK_DIM),
        'b_v': b_v, 'b_z': b_z,
        'c_q': c_q, 'c_k': c_k, 'c_v': c_v, 'c_z': c_z,
        'd_q': d_q.reshape(B, T, D_KV_HEADS, D_HEADS // D_KV_HEADS, HEAD_DIM), 'd_k': d_k, 'd_v': d_v, 'd_z': d_z,
    }


def block_attention(q, k, v, sink=None):
    B, S, Hkv, G, dk = q.shape
    nb = S // Q_BLOCK
    scale = 1.0 / math.sqrt(dk)
    qb = jnp.moveaxis(q.reshape(B, nb, Q_BLOCK, Hkv, G, dk), 1, 0)
    n_keys = k.shape[1]

    def one(qi):
        s = jnp.einsum('bqhgd,bkhd->bhgqk', qi, k, preferred_element_type=jnp.float32) * scale
        if sink is not None:
            s_sink = jnp.broadcast_to(sink.astype(jnp.float32)[None, :, :, None, None], s.shape[:-1] + (1,))
            s = jnp.concatenate([s, s_sink], axis=-1)
        p = jax.nn.softmax(s, axis=-1)[..., :n_keys]
        return jnp.einsum('bhgqk,bkhd->bqhgd', p.astype(v.dtype), v)

    o = lax.map(one, qb)
    return jnp.moveaxis(o, 0, 1).reshape(B, S, Hkv, G, v.shape[-1])


def windowed_attention(q, k, v, k_ctx, v_ctx, sink):
    B, S, Hkv, G, d = q.shape
    nb = S // Q_BLOCK
    n_ctx = k_ctx.shape[1]
    scale = 1.0 / math.sqrt(d)
    qb = q.reshape(B, nb, Q_BLOCK, Hkv, G, d)
    pad = [(0, 0), (Q_BLOCK, Q_BLOCK), (0, 0), (0, 0)]
    kb = jnp.pad(k, pad).reshape(B, nb + 2, Q_BLOCK, Hkv, d)
    vb = jnp.pad(v, pad).reshape(B, nb + 2, Q_BLOCK, Hkv, d)
    k_band = jnp.concatenate([kb[:, :-2], kb[:, 1:-1], kb[:, 2:]], axis=2)
    v_band = jnp.concatenate([vb[:, :-2], vb[:, 1:-1], vb[:, 2:]], axis=2)
    blk = jnp.arange(nb)[:, None]
    qpos = blk * Q_BLOCK + jnp.arange(Q_BLOCK)[None, :]
    kpos = (blk - 1) * Q_BLOCK + jnp.arange(3 * Q_BLOCK)[None, :]
    valid = ((jnp.abs(qpos[:, :, None] - kpos[:, None, :]) <= WINDOW)
             & (kpos >= 0)[:, None, :] & (kpos < S)[:, None, :])
    s_band = jnp.einsum('bnqhgd,bnkhd->bnhgqk', qb, k_band, preferred_element_type=jnp.float32) * scale
    s_band = jnp.where(valid[None, :, None, None], s_band, NEG_INF)
    s_ctx = jnp.einsum('bnqhgd,bkhd->bnhgqk', qb, k_ctx, preferred_element_type=jnp.float32) * scale
    s_sink = jnp.broadcast_to(sink.astype(jnp.float32)[None, None, :, :, None, None], s_ctx.shape[:-1] + (1,))
    p = jax.nn.softmax(jnp.concatenate([s_ctx, s_band, s_sink], axis=-1), axis=-1)
    p_ctx = p[..., :n_ctx].astype(v.dtype)
    p_band = p[..., n_ctx:n_ctx + 3 * Q_BLOCK].astype(v.dtype)
    o = (jnp.einsum('bnhgqk,bkhd->bnqhgd', p_ctx, v_ctx)
         + jnp.einsum('bnhgqk,bnkhd->bnqhgd', p_band, v_band))
    return o.reshape(B, S, Hkv, G, d)


def diff_block_attention(q, k, v, lam):
    B, S, H, _, dk = q.shape
    nb = S // Q_BLOCK
    scale = 1.0 / math.sqrt(dk)
    qb = jnp.moveaxis(q.reshape(B, nb, Q_BLOCK, H, 2, dk), 1, 0)

    def one(qi):
        s = jnp.einsum('bqhcd,bkhcd->bhcqk', qi, k, preferred_element_type=jnp.float32) * scale
        p = jax.nn.softmax(s, axis=-1)
        a = p[:, :, 0] - lam * p[:, :, 1]
        return jnp.einsum('bhqk,bkhd->bqhd', a.astype(v.dtype), v)

    o = lax.map(one, qb)
    return jnp.moveaxis(o, 0, 1).reshape(B, S, H, v.shape[-1])


def mixers(p, pc, sink, lam, lam_init, subln_g, latent):
    B, T = p['a_z'].shape[:2]
    if latent:
        a_o = windowed_attention(p['a_q'], p['a_k'], p['a_v'], pc['a_k'], pc['a_v'], sink)
        keys = lambda name: jnp.concatenate([pc[name], p[name]], axis=1)
    else:
        a_o = block_attention(p['a_q'], p['a_k'], p['a_v'], sink=sink)
        keys = lambda name: pc[name]
    b_o = diff_block_attention(p['b_q'], keys('b_k'), keys('b_v'), lam)
    b_o = rms_norm(b_o, subln_g) * (1.0 - lam_init)
    c_o = block_attention(p['c_q'], keys('c_k'), keys('c_v'))
    d_o = block_attention(p['d_q'], keys('d_k'), keys('d_v'))
    return jnp.concatenate([
        a_o.reshape(B, T, GROUP_WIDTH) * jax.nn.silu(p['a_z']),
        b_o.reshape(B, T, GROUP_WIDTH) * jax.nn.silu(p['b_z']),
        c_o.reshape(B, T, GROUP_WIDTH) * jax.nn.silu(p['c_z']),
        d_o.reshape(B, T, GROUP_WIDTH) * jax.nn.silu(p['d_z']),
    ], axis=-1)


def setup_inputs(seed: int = 0) -> dict:
    key = jax.random.key(seed)
    ks = jax.random.split(key, 19)
    f32 = jnp.float32

    def nrm(k, shape, scale):
        return jax.random.normal(k, shape, f32) * scale

    def gain(k, shape):
        return 1.0 + 0.05 * jax.random.normal(k, shape, f32)

    return {
        'x': nrm(ks[0], (BATCH, SEQ, D_MODEL), 1.0),
        'c': nrm(ks[1], (BATCH, D_MODEL), 1.0),
        'ctx': nrm(ks[2], (BATCH, CTX_LEN, D_MODEL), 1.0),
        'c_ctx': nrm(ks[3], (D_MODEL,), 1.0),
        'w_mod': nrm(ks[4], (DEPTH, D_MODEL, 3 * D_MODEL), 0.2 * D_MODEL ** -0.5),
        'b_mod': nrm(ks[5], (DEPTH, 3 * D_MODEL), 0.02),
        'norm_g': gain(ks[6], (DEPTH, D_MODEL)),
        'w_in': nrm(ks[7], (DEPTH, D_MODEL, IN_WIDTH), D_MODEL ** -0.5),
        'c_q_norm_g': gain(ks[8], (DEPTH, C_Q_LORA)),
        'c_kv_norm_g': gain(ks[9], (DEPTH, C_KV_LORA)),
        'c_w_uq': nrm(ks[10], (DEPTH, C_Q_LORA, C_HEADS * (C_NOPE + C_ROPE)), C_Q_LORA ** -0.5),
        'c_w_ukv': nrm(ks[11], (DEPTH, C_KV_LORA, C_HEADS * (C_NOPE + C_V)), C_KV_LORA ** -0.5),
        'd_q_norm_g': gain(ks[12], (DEPTH, HEAD_DIM)),
        'd_k_norm_g': gain(ks[13], (DEPTH, HEAD_DIM)),
        'a_sink': nrm(ks[14], (DEPTH, A_HEADS), 0.5),
        'b_lambda': nrm(ks[15], (DEPTH, 4, B_QK_DIM), 0.1),
        'b_subln_g': gain(ks[16], (DEPTH, B_V_DIM)),
        'w_out': nrm(ks[17], (DEPTH, MIX_WIDTH, D_MODEL), MIX_WIDTH ** -0.5),
        'final_norm_g': gain(ks[18], (D_MODEL,)),
    }


def reference(x, c, ctx, c_ctx, w_mod, b_mod, norm_g, w_in, c_q_norm_g, c_kv_norm_g,
              c_w_uq, c_w_ukv, d_q_norm_g, d_k_norm_g, a_sink, b_lambda, b_subln_g,
              w_out, final_norm_g):
    B, S, _ = x.shape
    ROWS = S // GRID_W
    rope_h = axial_rope_tables(ROWS, HEAD_DIM)
    rope_b = axial_rope_tables(ROWS, B_QK_DIM)
    rope_c = axial_rope_tables(ROWS, C_ROPE)
    for l in range(DEPTH):
        last = l == DEPTH - 1
        shift_x, scale_x, gate_x = jnp.split(jax.nn.silu(c) @ w_mod[l] + b_mod[l], 3, axis=-1)
        shift_c, scale_c, gate_c = jnp.split(jax.nn.silu(c_ctx) @ w_mod[l] + b_mod[l], 3, axis=-1)
        hx = rms_norm(x, norm_g[l]) * (1.0 + scale_x[:, None, :]) + shift_x[:, None, :]
        hc = rms_norm(ctx, norm_g[l]) * (1.0 + scale_c) + shift_c
        lw = (w_in[l], c_q_norm_g[l], c_kv_norm_g[l], c_w_uq[l], c_w_ukv[l], d_q_norm_g[l], d_k_norm_g[l])
        px = branch_inputs(hx, *lw, ropes=(rope_h, rope_b, rope_c))
        pc = branch_inputs(hc, *lw, ropes=None)
        lam_init = 0.8 - 0.6 * math.exp(-0.3 * l)
        lq1, lk1, lq2, lk2 = b_lambda[l].astype(jnp.float32)
        lam = jnp.exp(jnp.sum(lq1 * lk1)) - jnp.exp(jnp.sum(lq2 * lk2)) + lam_init
        sink = a_sink[l].reshape(A_KV_HEADS, A_HEADS // A_KV_HEADS)
        ux = mixers(px, pc, sink, lam, lam_init, b_subln_g[l], latent=True)
        if not last:
            uc = mixers(pc, pc, sink, lam, lam_init, b_subln_g[l], latent=False)
            ctx = ctx + gate_c * (uc @ w_out[l])
        x = x + gate_x[:, None, :] * (ux @ w_out[l])
    return rms_norm(x, final_norm_g)
```

```python
import math
from contextlib import ExitStack
import numpy as np
import ml_dtypes
import concourse.bass as bass
import concourse.mybir as mybir
from concourse.bass_utils import run_bass_kernel_spmd

F32 = mybir.dt.float32
BF16 = mybir.dt.bfloat16
AF = mybir.ActivationFunctionType
ALU = mybir.AluOpType
AX = mybir.AxisListType
PE, ACT, DVE, POOL, SP = "tensor", "scalar", "vector", "gpsimd", "sync"
ENGS = (PE, ACT, DVE, POOL, SP)

D = 2048
T = 4352
NT = 34
NC_T = 2
DEPTH = 2
INW = 6272
EPS = 1e-6
GROUPS = [(0, 512), (512, 512), (1024, 512), (1536, 512), (2048, 512), (2560, 512), (3072, 512),
          (3584, 448), (4032, 192), (4224, 512), (4736, 512), (5248, 512), (5760, 512)]


_UID = [0]
P2_ONLY = None
G8SKIP = None


def I(method, *args, **kwargs):
    return (method, args, kwargs)


class Slot:
    __slots__ = ("w", "r")

    def __init__(self):
        self.w = None
        self.r = []


class Prog:
    def __init__(self, nc):
        self.nc = nc
        self.streams = {e: [] for e in ENGS}
        self.sems = {}
        self.cnt = {}
        self.seen = {e: {} for e in ENGS}
        self._stack = []
        self.pool = []
        self.pool_sw = []
        self.kind = {}
        for e in (PE, ACT, DVE, POOL):
            self.newsem("E_" + e)

    def newsem(self, name):
        cm = self.nc.semaphore(name)
        s = cm.__enter__()
        self._stack.append(cm)
        self.sems[name] = s
        self.cnt[name] = 0
        return name

    def getsem(self, sw=False):
        pool = self.pool_sw if sw else self.pool
        if pool:
            return pool.pop()
        name = self.newsem(("dqs%d" if sw else "dq%d") % len(self.sems))
        self.kind[name] = sw
        return name

    def _emit_waits(self, eng, toks):
        need = {}
        for t in toks:
            if t is None:
                continue
            s, v = t
            if self.seen[eng].get(s, 0) >= v:
                continue
            if need.get(s, 0) < v:
                need[s] = v
        for s, v in need.items():
            self.seen[eng][s] = v
            self.streams[eng].append(("wait", s, v))

    def _deps(self, reads, writes, extra):
        toks = list(extra)
        for s in reads:
            toks.append(s.w)
        for s in writes:
            toks.append(s.w)
            toks.extend(s.r)
        return toks

    def _update(self, tok, reads, writes):
        for s in reads:
            s.r.append(tok)
        for s in writes:
            s.w = tok
            s.r = []

    def op(self, eng, fns, reads=(), writes=(), extra=()):
        if isinstance(fns, tuple):
            fns = [fns]
        self._emit_waits(eng, self._deps(reads, writes, extra))
        sname = "E_" + eng
        self.cnt[sname] += 1
        tok = (sname, self.cnt[sname])
        for f in fns[:-1]:
            self.streams[eng].append(("ins", f, None))
        self.streams[eng].append(("ins", fns[-1], (sname, 1)))
        self._update(tok, reads, writes)
        return tok

    def dma(self, q, out, in_, sem, reads=(), writes=(), extra=(), **kw):
        assert self.kind[sem] == (q == POOL), (sem, q)
        self._emit_waits(q, self._deps(reads, writes, extra))
        self.cnt[sem] += 16
        tok = (sem, self.cnt[sem])
        self.streams[q].append(("ins", ("dma_start", (), dict(out=out, in_=in_, **kw)), (sem, 16)))
        self._update(tok, reads, writes)
        return tok

    def barrier(self):
        toks = [(s, c) for s, c in self.cnt.items() if c > 0]
        for e in ENGS:
            self._emit_waits(e, toks)

    def flush(self):
        nc = self.nc
        streams, sems = self.streams, self.sems

        def run(eng, e):
            for it in streams[eng]:
                if it[0] == "wait":
                    e.wait_ge(sems[it[1]], it[2])
                else:
                    f = it[1]
                    ins = getattr(e, f[0])(*f[1], **f[2])
                    if it[2] is not None:
                        ins.then_inc(sems[it[2][0]], it[2][1])

        with nc.Block() as block:
            @block.sync
            def _(e):
                run(SP, e)

            @block.tensor
            def _(e):
                run(PE, e)

            @block.scalar
            def _(e):
                run(ACT, e)

            @block.vector
            def _(e):
                run(DVE, e)

            @block.gpsimd
            def _(e):
                run(POOL, e)
        self.streams = {e: [] for e in ENGS}

    def close(self):
        for cm in reversed(self._stack):
            cm.__exit__(None, None, None)


class Buf:
    def __init__(self, P, es, nc, name, shape, dt, n=1, psum=False, dmasem=True, sw=False):
        self.aps, self.slots, self.sems = [], [], []
        for i in range(n):
            _UID[0] += 1
            if psum:
                t = es.enter_context(nc.psum_tensor("%s%d_%d" % (name, i, _UID[0]), shape, dt))
            else:
                t = es.enter_context(nc.sbuf_tensor("%s%d_%d" % (name, i, _UID[0]), shape, dt))
            self.aps.append(t.ap())
            self.slots.append(Slot())
            self.sems.append(P.getsem(sw) if dmasem else None)
        self.n = n
        self.i = -1
        self.P = P
        self.sw = sw

    def nxt(self):
        self.i = (self.i + 1) % self.n
        return self.aps[self.i], self.slots[self.i], self.sems[self.i]

    def release(self):
        for s in self.sems:
            if s is not None:
                (self.P.pool_sw if self.sw else self.P.pool).append(s)


def build_program(debug=False, nlayers=DEPTH, stop_after=None):
    nc = bass.Bass("TRN2", target_bir_lowering=False)
    dbg = {}

    def din(name, shape, dt=F32):
        return nc.dram_tensor(name, list(shape), dt, kind="ExternalInput").ap()

    def dscr(name, shape, dt):
        if debug and (debug is True or name in debug):
            a = nc.dram_tensor(name, list(shape), dt, kind="ExternalOutput").ap()
            dbg[name] = a
            return a
        return nc.dram_tensor(name, list(shape), dt, kind="Internal").ap()

    xin = din("xin", [T, D])
    cT = din("cT", [128, 16, 2])
    w_mod = din("w_mod", [DEPTH, D, 3 * D])
    bmod_row = din("bmod_row", [DEPTH, 1, 3 * D])
    normg_col = din("normg_col", [DEPTH, 128, 16])
    w_in = din("w_in", [DEPTH, D, INW])
    cqg = din("cqg", [DEPTH, 1, 448])
    ckvg = din("ckvg", [DEPTH, 1, 128])
    w_uq = din("w_uq", [DEPTH, 448, 768])
    w_ukv = din("w_ukv", [DEPTH, 128, 1024])
    dqg = din("dqg", [DEPTH, 1, 128])
    dkg = din("dkg", [DEPTH, 1, 128])
    a_sink = din("a_sink", [DEPTH, 1, 4])
    b_lambda = din("b_lambda", [DEPTH, 1, 256])
    sublng = din("sublng", [DEPTH, 1, 128])
    w_out = din("w_out", [DEPTH, D, D])
    fng = din("fng", [1, D])
    ropeH = din("ropeH", [2, 4096, 128])
    ropeB = din("ropeB", [2, 4096, 64])
    identf = din("identf", [128, 128])
    identb = din("identb", [128, 128], BF16)
    sel = din("sel", [2, 2, 128])
    amask = din("amask", [6, 128, 512], BF16)
    out = nc.dram_tensor("out", [4096, D], F32, kind="ExternalOutput").ap()

    WB = dscr("WB", [D, INW], BF16)
    QT = {"A": dscr("QT_A", [4, 128, T], BF16), "B": dscr("QT_B", [4, 128, T], BF16),
          "Cn": dscr("QT_Cn", [4, 128, T], BF16), "Cr": dscr("QT_Cr", [4, 64, T], BF16),
          "D": dscr("QT_D", [4, 128, T], BF16)}
    KT = {"A": dscr("KT_A", [2, 128, T], BF16), "B": dscr("KT_B", [4, 128, T], BF16),
          "Cn": dscr("KT_Cn", [4, 128, T], BF16), "Cr": dscr("KT_Cr", [1, 64, T], BF16),
          "D": dscr("KT_D", [2, 128, T], BF16)}
    VV = {"A": dscr("V_A", [T, 256], BF16), "B": dscr("V_B", [T, 512], BF16),
          "C": dscr("V_C", [T, 512], BF16), "D": dscr("V_D", [T, 256], BF16)}
    WOB = dscr("WOB", [D, D], BF16)
    ZZ = dscr("ZZ", [T, D], BF16)
    UU = dscr("UU", [T, D], BF16)
    XS = dscr("XS", [T, D], F32)
    if debug:
        HT_dbg = dscr("HT_dbg", [128, 16, T], BF16)
        MOD_dbg = dscr("MOD_dbg", [128, 64 + 2 * D], F32)
    dbg_ht = bool(debug) and (debug is True or "HT_dbg" in debug)
    dbg_mod = bool(debug) and (debug is True or "MOD_dbg" in debug)

    P = Prog(nc)
    top = ExitStack()

    def sb(es, name, shape, dt):
        _UID[0] += 1
        return es.enter_context(nc.sbuf_tensor("%s_%d" % (name, _UID[0]), list(shape), dt)).ap()

    c_identf = sb(top, "c_identf", [128, 128], F32)
    c_identb = sb(top, "c_identb", [128, 128], BF16)
    c_sel = sb(top, "c_sel", [2, 2, 128], F32)
    c_amask = sb(top, "c_amask", [128, 6, 512], BF16)
    modA = sb(top, "modA", [128, 2, 16], F32)
    modB = sb(top, "modB", [128, 2, 16], F32)
    gate_bc = sb(top, "gate_bc", [128, 2, D], F32)
    g_cq = sb(top, "g_cq", [128, 448], F32)
    g_ckv = sb(top, "g_ckv", [128, 128], F32)
    g_dq = sb(top, "g_dq", [128, 128], F32)
    g_dk = sb(top, "g_dk", [128, 128], F32)
    g_sub = sb(top, "g_sub", [128, 128], F32)
    esink = sb(top, "esink", [128, 4], F32)
    lamv = sb(top, "lamv", [128, 4], F32)
    wuq_b = sb(top, "wuq_b", [128, 4, 768], BF16)
    wukv_b = sb(top, "wukv_b", [128, 1024], BF16)
    S_const = Slot()
    S_mod = Slot()
    S_lw = Slot()
    csem = P.getsem()
    P.dma(SP, c_identf, identf, csem, writes=[S_const])
    P.dma(SP, c_identb, identb, csem, writes=[S_const])
    P.dma(SP, c_sel, sel, csem, writes=[S_const])
    P.dma(SP, c_amask, amask.rearrange("m p q -> p m q"), csem, writes=[S_const])
    P.barrier()
    P.flush()

    def rstd_ops(ss, n, scale, rd, wr):
        P.op(ACT, I("activation", ss, ss, AF.Sqrt, bias=c_eps, scale=scale), reads=rd, writes=wr)
        P.op(DVE, I("reciprocal", ss, ss), reads=wr, writes=wr)

    c_eps = sb(top, "c_eps", [128, 1], F32)
    P.op(DVE, I("memset", c_eps, EPS), writes=[S_const])

    for l in range(nlayers):
        last = (l == DEPTH - 1)
        lam_init = 0.8 - 0.6 * math.exp(-0.3 * l)
        xsrc = xin if l == 0 else XS

        with ExitStack() as es:
            sc = sb(es, "sc", [128, 16, 2], F32)
            bcol = sb(es, "bcol", [128, 32], F32)
            gcol = sb(es, "gcol", [128, 16], F32)
            modc = sb(es, "modc", [128, 32, 2], F32)
            mrow = sb(es, "mrow", [2, 3 * D], F32)
            brow = sb(es, "brow", [2, 3 * D], F32)
            lamb = sb(es, "lamb", [128, 256], F32)
            lamt = sb(es, "lamt", [128, 256], F32)
            wst = sb(es, "wst", [128, 4, 768], F32)
            wst2 = sb(es, "wst2", [128, 1024], F32)
            WM = Buf(P, es, nc, "wm", [128, 16, 512], F32, n=2)
            ps_col = Buf(P, es, nc, "ps_col", [128, 32, 2], F32, n=1, psum=True, dmasem=False)
            ps_row = Buf(P, es, nc, "ps_row", [2, 512], F32, n=2, psum=True, dmasem=False)
            ps_bc = Buf(P, es, nc, "ps_bc", [128, 512], F32, n=2, psum=True, dmasem=False)
            S_sc, S_b, S_row, S_st = Slot(), Slot(), Slot(), Slot()
            sm = P.getsem()
            P.dma(SP, sc, cT, sm, writes=[S_sc])
            P.dma(SP, gcol, normg_col[l], sm, writes=[S_b])
            P.dma(SP, brow, bmod_row[l].partition_broadcast(2), sm, writes=[S_b])
            P.dma(SP, lamb, b_lambda[l].partition_broadcast(128), sm, writes=[S_b])
            P.dma(SP, esink, a_sink[l].partition_broadcast(128), sm, writes=[S_lw])
            P.dma(SP, g_cq, cqg[l].partition_broadcast(128), sm, writes=[S_lw])
            P.dma(SP, g_ckv, ckvg[l].partition_broadcast(128), sm, writes=[S_lw])
            P.dma(SP, g_dq, dqg[l].partition_broadcast(128), sm, writes=[S_lw])
            P.dma(SP, g_dk, dkg[l].partition_broadcast(128), sm, writes=[S_lw])
            P.dma(SP, g_sub, sublng[l].partition_broadcast(128), sm, writes=[S_lw])
            P.dma(SP, wst[:, 0:3, :], w_uq[l, 0:384, :].rearrange("(c p) n -> p c n", p=128), sm, writes=[S_st])
            P.dma(SP, wst[0:64, 3, :], w_uq[l, 384:448, :], sm, writes=[S_st])
            P.dma(SP, wst2, w_ukv[l], sm, writes=[S_st])
            tokall = (sm, P.cnt[sm])
            for s_ in (S_sc, S_b, S_lw, S_st):
                s_.w = tokall
            P.op(DVE, I("tensor_copy", wuq_b[:, 0:3, :], wst[:, 0:3, :]), reads=[S_st], writes=[S_lw])
            P.op(DVE, I("tensor_copy", wuq_b[0:64, 3, :], wst[0:64, 3, :]), reads=[S_st], writes=[S_lw])
            P.op(DVE, I("tensor_copy", wukv_b, wst2), reads=[S_st], writes=[S_lw])
            P.op(ACT, I("activation", sc, sc, AF.Silu), reads=[S_sc], writes=[S_sc])
            P.op(ACT, I("activation", esink, esink, AF.Exp), reads=[S_lw], writes=[S_lw])
            P.op(DVE, I("tensor_scalar", g_sub, g_sub, 1.0 - lam_init, None, op0=ALU.mult), reads=[S_lw], writes=[S_lw])
            lv = lamb.rearrange("p (a b) -> p a b", a=4)
            lt = lamt.rearrange("p (a b) -> p a b", a=4)
            P.op(DVE, I("tensor_tensor", lt[:, 0, :], lv[:, 0, :], lv[:, 1, :], ALU.mult), reads=[S_b], writes=[S_row])
            P.op(DVE, I("tensor_tensor", lt[:, 1, :], lv[:, 2, :], lv[:, 3, :], ALU.mult), reads=[S_b], writes=[S_row])
            P.op(DVE, I("tensor_reduce", lamv[:, 1:3], lt[:, 0:2, :], AX.X, ALU.add), reads=[S_row], writes=[S_lw])
            P.op(ACT, I("activation", lamv[:, 1:3], lamv[:, 1:3], AF.Exp), reads=[S_lw], writes=[S_lw])
            P.op(DVE, I("tensor_tensor", lamv[:, 3:4], lamv[:, 2:3], lamv[:, 1:2], ALU.subtract), reads=[S_lw], writes=[S_lw])
            P.op(DVE, I("tensor_scalar", lamv[:, 0:1], lamv[:, 3:4], -lam_init, None, op0=ALU.add), reads=[S_lw], writes=[S_lw])
            for cg in range(12):
                wt, sw, semw = WM.nxt()
                P.dma(SP if cg % 2 == 0 else ACT, wt, w_mod[l, :, cg * 512:(cg + 1) * 512].rearrange("(c p) n -> p c n", p=128), semw, writes=[sw])
                pr, spr, _ = ps_row.nxt()
                fns = [I("matmul", pr, sc[:, kc, :], wt[:, kc, :], start=(kc == 0), stop=(kc == 15)) for kc in range(16)]
                P.op(PE, fns, reads=[sw, S_sc], writes=[spr])
                c0 = cg * 512
                P.op(DVE, I("tensor_tensor", mrow[:, c0:c0 + 512], pr, brow[:, c0:c0 + 512], ALU.add),
                     reads=[spr, S_b], writes=[S_row])
            pcol, scol, _ = ps_col.nxt()
            fns = [I("transpose", pcol[:, ch, :], mrow[0:2, ch * 128:(ch + 1) * 128], c_identf[0:2, 0:2]) for ch in range(32)]
            P.op(PE, fns, reads=[S_row, S_const], writes=[scol])
            for v in range(2):
                P.op(DVE, I("scalar_tensor_tensor", modA[:, v, :], pcol[:, 16:32, v], 1.0, gcol, op0=ALU.add, op1=ALU.mult),
                     reads=[scol, S_b], writes=[S_mod])
                P.op(DVE, I("tensor_copy", modB[:, v, :], pcol[:, 0:16, v]), reads=[scol], writes=[S_mod])
            grow = mrow[:, 4096:6144]
            for v in range(2):
                for cq in range(4):
                    pb, spb, _ = ps_bc.nxt()
                    P.op(PE, I("matmul", pb, c_sel[:, v, :], grow[:, cq * 512:(cq + 1) * 512], start=True, stop=True),
                         reads=[S_row, S_const], writes=[spb])
                    P.op(ACT, I("copy", gate_bc[:, v, cq * 512:(cq + 1) * 512], pb), reads=[spb], writes=[S_mod])
            if dbg_mod and l == 0:
                sd = Slot()
                P.dma(SP, MOD_dbg[:, 0:32], modA.rearrange("p a b -> p (a b)"), sm, reads=[S_mod], writes=[sd])
                P.dma(SP, MOD_dbg[:, 32:64], modB.rearrange("p a b -> p (a b)"), sm, reads=[S_mod], writes=[sd])
                P.dma(SP, MOD_dbg[:, 64:64 + 2 * D], gate_bc.rearrange("p a b -> p (a b)"), sm, reads=[S_mod], writes=[sd])
            P.barrier()
            P.flush()
            WM.release()
            P.pool.append(sm)
        if stop_after == "P0":
            break

        esr = ExitStack()
        c_ropeH = sb(esr, "c_ropeH", [128, 2, 32, 128], F32)
        c_ropeB = sb(esr, "c_ropeB", [128, 2, 32, 64], F32)
        rsem = P.getsem()
        P.dma(SP, c_ropeH, ropeH.rearrange("c (t p) d -> p c t d", p=128), rsem, writes=[S_const])
        P.dma(SP, c_ropeB, ropeB.rearrange("c (t p) d -> p c t d", p=128), rsem, writes=[S_const])
        P.barrier()
        P.pool.append(rsem)
        PASSES = [list(range(0, 9)), list(range(9, 18)), list(range(18, 26)), list(range(26, 34))]
        for hf, tiles in enumerate(PASSES):
            with ExitStack() as es:
                hT = sb(es, "hT", [128, 16, 9 * 128], BF16)
                S_h = [Slot() for _ in range(9)]
                with ExitStack() as es1:
                    XB = Buf(P, es1, nc, "xb", [128, D], F32, n=3)
                    XSC = Buf(P, es1, nc, "xsc", [128, D], F32, n=2, dmasem=False)
                    JK = Buf(P, es1, nc, "jk", [128, D], BF16, n=2, dmasem=False)
                    ST = Buf(P, es1, nc, "st1", [128, 1], F32, n=3, dmasem=False)
                    PT = Buf(P, es1, nc, "pt1", [128, 4, 128], F32, n=4, psum=True, dmasem=False)
                    def p1a(ti, tt):
                        xt, sx, semx = XB.nxt()
                        P.dma(SP, xt, xsrc[tt * 128:(tt + 1) * 128, :], semx, writes=[sx])
                        jk, sj, _ = JK.nxt()
                        st, sst, _ = ST.nxt()
                        P.op(ACT, I("activation", jk, xt, AF.Square, scale=1.0 / math.sqrt(D), accum_out=st),
                             reads=[sx], writes=[sj, sst])
                        rstd_ops(st, 1, 1.0, [sst, S_const], [sst])
                        xs, sxs, _ = XSC.nxt()
                        P.op(DVE, I("tensor_scalar", xs, xt, st, None, op0=ALU.mult), reads=[sx, sst], writes=[sxs])
                        return xs, sxs

                    def p1b(ti, tt, xs, sxs):
                        v = 1 if tt < NC_T else 0
                        for q4 in range(4):
                            pt, spt, _ = PT.nxt()
                            fns = [I("transpose", pt[:, j, :], xs[:, (q4 * 4 + j) * 128:(q4 * 4 + j + 1) * 128], c_identf)
                                   for j in range(4)]
                            P.op(PE, fns, reads=[sxs, S_const], writes=[spt])
                            for j in range(4):
                                ch = q4 * 4 + j
                                dst = hT[:, ch, ti * 128:(ti + 1) * 128]
                                if j % 2 == 0:
                                    P.op(DVE, I("tensor_scalar", dst, pt[:, j, :], modA[:, v, ch:ch + 1], modB[:, v, ch:ch + 1],
                                                op0=ALU.mult, op1=ALU.add), reads=[spt, S_mod], writes=[S_h[ti]])
                                else:
                                    P.op(ACT, I("activation", dst, pt[:, j, :], AF.Identity, bias=modB[:, v, ch:ch + 1],
                                                scale=modA[:, v, ch:ch + 1]), reads=[spt, S_mod], writes=[S_h[ti]])

                    prevA = None
                    for ti, tt in enumerate(tiles):
                        curA = p1a(ti, tt)
                        if prevA is not None:
                            p1b(ti - 1, tiles[ti - 1], *prevA)
                        prevA = curA
                    p1b(len(tiles) - 1, tiles[-1], *prevA)
                    if dbg_ht and l == 0:
                        sd = Slot()
                        P.dma(SP, HT_dbg[:, :, tiles[0] * 128:(tiles[-1] + 1) * 128], hT[:, :, 0:len(tiles) * 128], P.getsem(), reads=S_h, writes=[sd])
                    P.barrier()
                    P.flush()
                    XB.release()
                if stop_after == "P1":
                    continue
                with ExitStack() as es2:
                    WBF = Buf(P, es2, nc, "wbf", [128, 16, 512], BF16, n=2)
                    WF = Buf(P, es2, nc, "wf", [128, 16, 256], F32, n=1) if hf == 0 else None
                    WOS = Buf(P, es2, nc, "wos", [128, 16, 256], BF16, n=1) if hf == 0 else None

                    wo_state = {}

                    def wout_load(i):
                        wf, swf, semf = WF.nxt()
                        P.dma(SP, wf, w_out[l, :, i * 256:(i + 1) * 256].rearrange("(c p) n -> p c n", p=128), semf, writes=[swf])
                        wo_state[i] = (wf, swf)

                    def wout_conv(i):
                        wf, swf = wo_state.pop(i)
                        ws_, sws_, semws_ = WOS.nxt()
                        P.op(DVE, I("tensor_copy", ws_, wf), reads=[swf], writes=[sws_])
                        P.dma(SP, WOB[:, i * 256:(i + 1) * 256].rearrange("(c p) n -> p c n", p=128), ws_, semws_, reads=[sws_])
                    PY = Buf(P, es2, nc, "py", [128, 512], F32, n=2, psum=True, dmasem=False)
                    PY2 = Buf(P, es2, nc, "py2", [128, 1024], F32, n=1, psum=True, dmasem=False)
                    PTR = Buf(P, es2, nc, "ptr", [128, 8, 128], BF16, n=2, psum=True, dmasem=False)
                    Y = Buf(P, es2, nc, "y", [128, 512], F32, n=2, dmasem=False)
                    Y2 = Buf(P, es2, nc, "y2", [128, 1024], F32, n=1, dmasem=False)
                    T1 = Buf(P, es2, nc, "t1", [128, 512], F32, n=2, dmasem=False)
                    T2 = Buf(P, es2, nc, "t2", [128, 512], F32, n=2, dmasem=False)
                    YB = Buf(P, es2, nc, "yb", [128, 512], BF16, n=6, sw=True)
                    STG = Buf(P, es2, nc, "stg", [128, 4, 128], BF16, n=4)
                    CT = Buf(P, es2, nc, "ct", [128, 4, 128], BF16, n=2, dmasem=False)
                    SS = Buf(P, es2, nc, "ss", [128, 4], F32, n=2, dmasem=False)

                    def evac(dst, src, rd, wr, eng=None):
                        if eng == ACT:
                            return P.op(ACT, I("copy", dst, src), reads=rd, writes=wr)
                        return P.op(DVE, I("tensor_copy", dst, src), reads=rd, writes=wr)

                    def rope(y, sy, W, tab, q, tt, outb, sob):
                        lt = tt - NC_T
                        wd = min(W, tab.shape[-1])
                        nb = W // wd
                        cosv = tab[:, 0, lt, 0:wd]
                        sinv = tab[:, 1, lt, 0:wd]
                        t1, st1, _ = T1.nxt()
                        t2, st2, _ = T2.nxt()
                        yv = y[:, 0:W].rearrange("p (h d) -> p h d", d=wd)
                        t1v = t1[:, 0:W].rearrange("p (h d) -> p h d", d=wd)
                        t2v = t2[:, 0:W].rearrange("p (h d) -> p h d", d=wd)
                        cb = cosv.unsqueeze(1).broadcast_to([128, nb, wd])
                        P.op(DVE, I("tensor_tensor", t1v, yv, cb, ALU.mult), reads=[sy, S_const], writes=[st1])
                        yq = y[:, 0:W].rearrange("p (h g b q) -> p h g b q", h=nb, b=2, q=q)
                        tq = t2[:, 0:W].rearrange("p (h g b q) -> p h g b q", h=nb, b=2, q=q)
                        sq = sinv.rearrange("p (g b q) -> p g b q", b=2, q=q)
                        for b in range(2):
                            fn = I("tensor_tensor", tq[:, :, :, b, :], yq[:, :, :, 1 - b, :],
                                   sq[:, :, b, :].unsqueeze(1).broadcast_to([128, nb, wd // (2 * q), q]), ALU.mult)
                            P.op(DVE, fn, reads=[sy, S_const], writes=[st2])
                        P.op(DVE, I("tensor_tensor", outb[:, 0:W], t1[:, 0:W], t2[:, 0:W], ALU.add), reads=[st1, st2], writes=[sob])

                    def to_T(yb, syb, blocks, tt):
                        for i0 in range(0, len(blocks), 4):
                            blk = blocks[i0:i0 + 4]
                            pt, spt, _ = PTR.nxt()
                            fns = [I("transpose", pt[:, j, :], yb[:, c0:c0 + 128], c_identb) for j, (c0, _d) in enumerate(blk)]
                            P.op(PE, fns, reads=[syb, S_const], writes=[spt])
                            sg, ssg, semg = STG.nxt()
                            P.op(ACT, I("copy", sg[:, 0:len(blk), :], pt[:, 0:len(blk), :]), reads=[spt], writes=[ssg])
                            for j, (c0, dsts) in enumerate(blk):
                                for (r0, nr, dst) in dsts:
                                    P.dma(ACT, dst[:, tt * 128:(tt + 1) * 128], sg[r0:r0 + nr, j, :], semg, reads=[ssg])

                    def store_tok(src, ssrc, sem, dst):
                        P.dma(POOL, dst, src, sem, reads=[ssrc])

                    def head_norm(y, sy, nh, g):
                        t1, st1, _ = T1.nxt()
                        ss, sss, _ = SS.nxt()
                        W = nh * 128
                        P.op(ACT, [I("activation", t1[:, h * 128:(h + 1) * 128], y[:, h * 128:(h + 1) * 128], AF.Square,
                                     accum_out=ss[:, h:h + 1]) for h in range(nh)], reads=[sy], writes=[st1, sss])
                        rstd_ops(ss[:, 0:nh], nh, 1.0 / 128, [sss, S_const], [sss])
                        for h in range(nh):
                            P.op(DVE, I("scalar_tensor_tensor", y[:, h * 128:(h + 1) * 128], y[:, h * 128:(h + 1) * 128],
                                        ss[:, h:h + 1], g, op0=ALU.mult, op1=ALU.mult), reads=[sss, sy, S_lw], writes=[sy])

                    def post(gi, tt, py, spy):
                        lat = tt >= NC_T
                        tsl = slice(tt * 128, (tt + 1) * 128)
                        if gi in (0, 3, 4, 10):
                            name = {0: ("A", QT), 3: ("B", QT), 4: ("B", KT), 10: ("D", QT)}[gi]
                            dstT = name[1][name[0]]
                            yb, syb, semyb = YB.nxt()
                            if gi == 10 or lat:
                                y, sy, _ = Y.nxt()
                                evac(y, py, [spy], [sy], ACT)
                                if gi == 10:
                                    head_norm(y, sy, 4, g_dq)
                                if lat:
                                    rope(y, sy, 512, c_ropeB if gi in (3, 4) else c_ropeH, 16 if gi in (3, 4) else 32, tt, yb, syb)
                                else:
                                    evac(yb, y, [sy], [syb])
                            else:
                                evac(yb, py, [spy], [syb], ACT)
                            yield
                            yield
                            if gi == 10:
                                yield
                            to_T(yb, syb, [(h * 128, [(0, 128, dstT[h])]) for h in range(4)], tt)
                        elif gi in (1, 11):
                            mx = "A" if gi == 1 else "D"
                            yb, syb, semyb = YB.nxt()
                            if gi == 11 or lat:
                                y, sy, _ = Y.nxt()
                                evac(y[:, 0:256], py[:, 0:256], [spy], [sy], ACT)
                                if gi == 11:
                                    head_norm(y, sy, 2, g_dk)
                                if lat:
                                    rope(y, sy, 256, c_ropeH, 32, tt, yb, syb)
                                else:
                                    evac(yb[:, 0:256], y[:, 0:256], [sy], [syb])
                            else:
                                evac(yb[:, 0:256], py[:, 0:256], [spy], [syb], ACT)
                            P.op(DVE, I("tensor_copy", yb[:, 256:512], py[:, 256:512]), reads=[spy], writes=[syb])
                            yield
                            yield
                            if gi == 11:
                                yield
                            to_T(yb, syb, [(h * 128, [(0, 128, KT[mx][h])]) for h in range(2)], tt)
                            store_tok(yb[:, 256:512], syb, semyb, VV[mx][tsl, :])
                        elif gi == 5:
                            yb, syb, semyb = YB.nxt()
                            evac(yb, py, [spy], [syb], ACT)
                            store_tok(yb, syb, semyb, VV["B"][tsl, :])
                        elif gi in (2, 6, 9, 12):
                            zc = {2: 0, 6: 512, 9: 1024, 12: 1536}[gi]
                            yb, syb, semyb = YB.nxt()
                            P.op(ACT, I("activation", yb, py, AF.Silu), reads=[spy], writes=[syb])
                            store_tok(yb, syb, semyb, ZZ[tsl, zc:zc + 512])
                        elif gi == 7:
                            y, sy, _ = Y.nxt()
                            evac(y[:, 0:448], py[:, 0:448], [spy], [sy], ACT)
                            t1, st1, _ = T1.nxt()
                            ss, sss, _ = SS.nxt()
                            P.op(DVE, I("tensor_tensor", t1[:, 0:448], y[:, 0:448], y[:, 0:448], ALU.mult), reads=[sy], writes=[st1])
                            P.op(DVE, I("tensor_reduce", ss[:, 0:1], t1[:, 0:448], AX.X, ALU.add), reads=[st1], writes=[sss])
                            rstd_ops(ss[:, 0:1], 1, 1.0 / 448, [sss, S_const], [sss])
                            yb, syb, semyb = YB.nxt()
                            P.op(DVE, I("scalar_tensor_tensor", yb[:, 0:448], y[:, 0:448], ss[:, 0:1], g_cq, op0=ALU.mult, op1=ALU.mult),
                                 reads=[sy, sss, S_lw], writes=[syb])
                            yield
                            pt, spt, _ = PTR.nxt()
                            fns = [I("transpose", pt[:, j, :], yb[:, j * 128:(j + 1) * 128], c_identb) for j in range(4)]
                            P.op(PE, fns, reads=[syb, S_const], writes=[spt])
                            ct, sct, _ = CT.nxt()
                            P.op(ACT, I("copy", ct[:, 0:3, :], pt[:, 0:3, :]), reads=[spt], writes=[sct])
                            P.op(ACT, I("copy", ct[0:64, 3, :], pt[0:64, 3, :]), reads=[spt], writes=[sct])
                            yield
                            p2, sp2, _ = PY2.nxt()
                            fns = []
                            for (c0, cw) in ((0, 512), (512, 256)):
                                for j in range(4):
                                    kk = 128 if j < 3 else 64
                                    fns.append(I("matmul", p2[:, c0:c0 + cw], ct[0:kk, j, :], wuq_b[0:kk, j, c0:c0 + cw],
                                                 start=(j == 0), stop=(j == 3)))
                            P.op(PE, fns, reads=[sct, S_lw], writes=[sp2])
                            y2, sy2, _ = Y2.nxt()
                            evac(y2[:, 0:512], p2[:, 0:512], [sp2], [sy2], ACT)
                            evac(y2[:, 512:768], p2[:, 512:768], [sp2], [sy2])
                            y2v = y2[:, 0:768].rearrange("p (h d) -> p h d", d=192)
                            ybn, sybn, _ = YB.nxt()
                            P.op(DVE, I("tensor_copy", ybn.rearrange("p (h d) -> p h d", d=128), y2v[:, :, 0:128]), reads=[sy2], writes=[sybn])
                            ybr, sybr, _ = YB.nxt()
                            if lat:
                                y, sy, _ = Y.nxt()
                                P.op(DVE, I("tensor_copy", y[:, 0:256].rearrange("p (h d) -> p h d", d=64), y2v[:, :, 128:192]),
                                     reads=[sy2], writes=[sy])
                                rope(y, sy, 256, c_ropeB, 16, tt, ybr, sybr)
                            else:
                                P.op(DVE, I("tensor_copy", ybr[:, 0:256].rearrange("p (h d) -> p h d", d=64), y2v[:, :, 128:192]),
                                     reads=[sy2], writes=[sybr])
                            yield
                            to_T(ybn, sybn, [(h * 128, [(0, 128, QT["Cn"][h])]) for h in range(4)], tt)
                            to_T(ybr, sybr, [(0, [(0, 64, QT["Cr"][0]), (64, 64, QT["Cr"][1])]), (128, [(0, 64, QT["Cr"][2]), (64, 64, QT["Cr"][3])])], tt)
                        elif gi == 8:
                            y, sy, _ = Y.nxt()
                            evac(y[:, 0:192], py[:, 0:192], [spy], [sy], ACT)
                            t1, st1, _ = T1.nxt()
                            ss, sss, _ = SS.nxt()
                            P.op(DVE, I("tensor_tensor", t1[:, 0:128], y[:, 0:128], y[:, 0:128], ALU.mult), reads=[sy], writes=[st1])
                            P.op(DVE, I("tensor_reduce", ss[:, 0:1], t1[:, 0:128], AX.X, ALU.add), reads=[st1], writes=[sss])
                            rstd_ops(ss[:, 0:1], 1, 1.0 / 128, [sss, S_const], [sss])
                            yb, syb, semyb = YB.nxt()
                            P.op(DVE, I("scalar_tensor_tensor", yb[:, 0:128], y[:, 0:128], ss[:, 0:1], g_ckv, op0=ALU.mult, op1=ALU.mult),
                                 reads=[sy, sss, S_lw], writes=[syb])
                            ybr, sybr, _ = YB.nxt()
                            if lat:
                                y3, sy3, _ = Y.nxt()
                                P.op(DVE, I("tensor_copy", y3[:, 0:64], y[:, 128:192]), reads=[sy], writes=[sy3])
                                rope(y3, sy3, 64, c_ropeB, 16, tt, ybr, sybr)
                            else:
                                P.op(DVE, I("tensor_copy", ybr[:, 0:64], y[:, 128:192]), reads=[sy], writes=[sybr])
                            yield
                            to_T(ybr, sybr, [(0, [(0, 64, KT["Cr"][0])])], tt)
                            pt, spt, _ = PTR.nxt()
                            P.op(PE, I("transpose", pt[:, 0, :], yb[:, 0:128], c_identb), reads=[syb, S_const], writes=[spt])
                            ct, sct, _ = CT.nxt()
                            P.op(ACT, I("copy", ct[:, 0, :], pt[:, 0, :]), reads=[spt], writes=[sct])
                            yield
                            p2, sp2, _ = PY2.nxt()
                            fns = [I("matmul", p2[:, c0:c0 + 512], ct[:, 0, :], wukv_b[:, c0:c0 + 512], start=True, stop=True)
                                   for c0 in (0, 512)]
                            P.op(PE, fns, reads=[sct, S_lw], writes=[sp2])
                            y2, sy2, _ = Y2.nxt()
                            evac(y2[:, 0:512], p2[:, 0:512], [sp2], [sy2], ACT)
                            evac(y2[:, 512:1024], p2[:, 512:1024], [sp2], [sy2])
                            y2v = y2.rearrange("p (h d) -> p h d", d=256)
                            ybn, sybn, _ = YB.nxt()
                            P.op(DVE, I("tensor_copy", ybn.rearrange("p (h d) -> p h d", d=128), y2v[:, :, 0:128]), reads=[sy2], writes=[sybn])
                            ybv, sybv, semybv = YB.nxt()
                            P.op(DVE, I("tensor_copy", ybv.rearrange("p (h d) -> p h d", d=128), y2v[:, :, 128:256]), reads=[sy2], writes=[sybv])
                            yield
                            to_T(ybn, sybn, [(h * 128, [(0, 128, KT["Cn"][h])]) for h in range(4)], tt)
                            store_tok(ybv, sybv, semybv, VV["C"][tsl, :])

                    pending = []

                    def advance():
                        for g_ in list(pending):
                            try:
                                next(g_)
                            except StopIteration:
                                pending.remove(g_)

                    def prep_w(gi):
                        g0, gw = GROUPS[gi]
                        wb, swb, semwb = WBF.nxt()
                        ev = {}
                        if hf > 0:
                            ev[0] = [lambda: P.dma(SP, wb[:, :, 0:gw], WB[:, g0:g0 + gw].rearrange("(c p) n -> p c n", p=128), semwb, writes=[swb])]
                            return (wb, swb), ev
                        halves = [(c0, min(256, gw - c0)) for c0 in range(0, gw, 256)]
                        st = {}

                        def load(h):
                            c0, cw = halves[h]
                            wf, swf, semf = WF.nxt()
                            st[h] = (wf, swf)
                            P.dma(SP, wf[:, :, 0:cw], w_in[l, :, g0 + c0:g0 + c0 + cw].rearrange("(c p) n -> p c n", p=128), semf, writes=[swf])

                        def conv(h):
                            c0, cw = halves[h]
                            wf, swf = st[h]
                            P.op(DVE, I("tensor_copy", wb[:, :, c0:c0 + cw], wf[:, :, 0:cw]), reads=[swf], writes=[swb])

                        def store():
                            P.dma(SP, WB[:, g0:g0 + gw].rearrange("(c p) n -> p c n", p=128), wb[:, :, 0:gw], semwb, reads=[swb])

                        ev[0] = [lambda: load(0)]
                        if len(halves) > 1:
                            ev[2] = [lambda: conv(0), lambda: load(1)]
                            ev[4] = [lambda: conv(1), store]
                        else:
                            ev[2] = [lambda: conv(0), store]
                        return (wb, swb), ev

                    glist = [gi for gi in range(len(GROUPS)) if P2_ONLY is None or gi in P2_ONLY]
                    wnext, ev0 = prep_w(glist[0])
                    for k_ in sorted(ev0):
                        for f_ in ev0[k_]:
                            f_()
                    for gpos, gi in enumerate(glist):
                        g0, gw = GROUPS[gi]
                        wb, swb = wnext
                        evn = {}
                        if gpos + 1 < len(glist):
                            wnext, evn = prep_w(glist[gpos + 1])
                        for ti, tt in enumerate(tiles):
                            for f_ in evn.get(ti, []):
                                f_()
                            if hf == 0 and gpos < 8:
                                if ti == 5:
                                    wout_load(gpos)
                                if ti == 7:
                                    wout_conv(gpos)
                            py, spy, _ = PY.nxt()
                            fns = [I("matmul", py[:, 0:gw], hT[:, kc, ti * 128:(ti + 1) * 128], wb[:, kc, 0:gw],
                                     start=(kc == 0), stop=(kc == 15)) for kc in range(16)]
                            P.op(PE, fns, reads=[S_h[ti], swb], writes=[spy])
                            advance()
                            g_new = post(gi, tt, py, spy)
                            pending.append(g_new)
                            try:
                                next(g_new)
                            except StopIteration:
                                pending.remove(g_new)
                    while pending:
                        advance()
                    P.barrier()
                    P.flush()
                    for b_ in (WBF, YB, STG) + ((WF, WOS) if WF is not None else ()):
                        b_.release()
        esr.close()
        if stop_after in ("P1", "P2"):
            break

        with ExitStack() as es:
            KTB = Buf(P, es, nc, "ktb", [128, T], BF16, n=2)
            KRB = Buf(P, es, nc, "krb", [128, T], BF16, n=1)
            VB = Buf(P, es, nc, "vb", [128, NT, 129], BF16, n=2)
            QB = Buf(P, es, nc, "qb", [128, T], BF16, n=4)
            QRB = Buf(P, es, nc, "qrb", [128, T], BF16, n=2)
            ZT = Buf(P, es, nc, "zt", [128, 4, 128], BF16, n=3, sw=True)
            UT = Buf(P, es, nc, "ut", [128, 4, 128], BF16, n=3, sw=True)
            PTB = Buf(P, es, nc, "ptb", [128, 2, 512], BF16, n=3, dmasem=False)
            PS = Buf(P, es, nc, "ps", [128, 2, 512], F32, n=2, psum=True, dmasem=False)
            PO = Buf(P, es, nc, "po", [128, 4, 256], F32, n=2, psum=True, dmasem=False)
            RC = Buf(P, es, nc, "rc", [128, 8], F32, n=3, dmasem=False)
            OA = Buf(P, es, nc, "oa", [128, 4, 128], F32, n=2, dmasem=False)
            OB = Buf(P, es, nc, "ob", [128, 4, 128], F32, n=2, dmasem=False)
            OC = Buf(P, es, nc, "oc", [128, 4, 128], F32, n=2, dmasem=False)
            for i in range(2):
                P.op(DVE, I("memset", VB.aps[i][:, :, 128:129], 1.0), writes=[VB.slots[i]])
                P.op(DVE, I("memset", QRB.aps[i][64:128, :], 0.0), writes=[QRB.slots[i]])
            P.op(DVE, I("memset", KRB.aps[0][64:128, :], 0.0), writes=[KRB.slots[0]])
            chunks = ([] if last else [(0, 2)]) + [(256 + 512 * c, 4) for c in range(8)]

            def key_units(mixer, q0, nqs):
                if q0 < 256:
                    kbs = [(0, None), (1, None)]
                elif mixer == "A":
                    c = (q0 - 256) // 512
                    kbs = [(0, None), (1, None)]
                    for r in range(-1, 5):
                        lb = 4 * c + r
                        if 0 <= lb < 32:
                            kbs.append((2 + lb, r + 1))
                else:
                    kbs = [(k, None) for k in range(NT)]
                return [kbs[i:i + 2] for i in range(0, len(kbs), 2)]

            work = []

            def add_head(mixer, kparts, qparts, vb, svb, vh, ucol, scale, sink_col, bmode):
                for (q0, nqs) in chunks:
                    units = key_units(mixer, q0, nqs)
                    for ui, u in enumerate(units):
                        work.append(dict(mixer=mixer, parts=kparts_q(kparts, qparts), vb=vb, svb=svb, vh=vh, ucol=ucol,
                                         scale=scale, sink=sink_col, bmode=bmode, q0=q0, nqs=nqs, unit=u,
                                         first=(ui == 0), lastu=(ui == len(units) - 1)))

            def kparts_q(kparts, qparts):
                return list(zip(kparts, qparts))

            def load_k(src_ap, buf):
                ap, s, sem = buf.nxt()
                P.dma(SP, ap[0:src_ap.shape[0], :], src_ap, sem, writes=[s])
                return ap, s

            def load_v(mx, h):
                ap, s, sem = VB.nxt()
                P.dma(SP, ap[:, :, 0:128], VV[mx][:, h * 128:(h + 1) * 128].rearrange("(b p) d -> p b d", p=128), sem, writes=[s])
                return ap, s

            sA = 1.0 / math.sqrt(128.0)
            sB = 1.0 / math.sqrt(64.0)
            sC = 1.0 / math.sqrt(192.0)
            zero_done = [False, False, False]

            def emit_all():
                state = {}

                def qk(w):
                    ps, sps, _ = PS.nxt()
                    nq = w["nqs"] * 128
                    fns = []
                    rd = []
                    for bi, (kb, _m) in enumerate(w["unit"]):
                        np_ = len(w["parts"])
                        for pi, ((kap, sk, K), (qap, sq, _K)) in enumerate(w["parts"]):
                            fns.append(I("matmul", ps[:, bi, 0:nq], kap[0:K, kb * 128:(kb + 1) * 128], qap[0:K, w["q0"]:w["q0"] + nq],
                                         start=(pi == 0), stop=(pi == np_ - 1)))
                            rd += [sk, sq]
                    P.op(PE, fns, reads=rd, writes=[sps])
                    w["ps"], w["sps"] = ps, sps

                def ex(w):
                    pt, spt, _ = PTB.nxt()
                    nq = w["nqs"] * 128
                    nb = len(w["unit"])
                    if nq == 512 and nb == 2:
                        P.op(ACT, I("activation", pt.rearrange("p b q -> p (b q)"), w["ps"].rearrange("p b q -> p (b q)"), AF.Exp, scale=w["scale"]),
                             reads=[w["sps"]], writes=[spt])
                    else:
                        P.op(ACT, [I("activation", pt[:, bi, 0:nq], w["ps"][:, bi, 0:nq], AF.Exp, scale=w["scale"]) for bi in range(nb)],
                             reads=[w["sps"]], writes=[spt])
                    for bi, (kb, m) in enumerate(w["unit"]):
                        if m is not None:
                            P.op(DVE, I("tensor_tensor", pt[:, bi, 0:nq], pt[:, bi, 0:nq], c_amask[:, m, 0:nq], ALU.mult),
                                 reads=[spt, S_const], writes=[spt])
                    w["pt"], w["spt"] = pt, spt

                def pv(w):
                    key = (w["ucol"], w["bmode"], w["q0"])
                    if w["first"]:
                        po, spo, _ = PO.nxt()
                        state[key] = (po, spo)
                    po, spo = state[key]
                    fns = []
                    nb = len(w["unit"])
                    for qs in range(w["nqs"]):
                        for bi, (kb, _m) in enumerate(w["unit"]):
                            fns.append(I("matmul", po[:, qs, 0:129], w["pt"][:, bi, qs * 128:(qs + 1) * 128], w["vb"][:, kb, :],
                                         start=(w["first"] and bi == 0 and qs % 2 == 0), stop=(w["lastu"] and bi == nb - 1),
                                         skip_group_check=True))
                    P.op(PE, fns, reads=[w["spt"], w["svb"]], writes=[spo])
                    if w["lastu"]:
                        finalize(w, po, spo)

                def finalize(w, po, spo):
                    nqs = w["nqs"]
                    q0 = w["q0"]
                    rc, src, _ = RC.nxt()
                    for b0 in range(0, nqs, 2):
                        if w["sink"] is not None:
                            P.op(DVE, I("tensor_scalar", rc[:, b0:b0 + 2], po[:, b0:b0 + 2, 128], esink[:, w["sink"]:w["sink"] + 1], None, op0=ALU.add),
                                 reads=[spo, S_lw], writes=[src])
                        else:
                            P.op(DVE, I("tensor_copy", rc[:, b0:b0 + 2], po[:, b0:b0 + 2, 128]), reads=[spo], writes=[src])
                    P.op(DVE, I("reciprocal", rc[:, 0:nqs], rc[:, 0:nqs]), reads=[src], writes=[src])
                    tok = slice(q0, q0 + nqs * 128)
                    if w["bmode"] == 0:
                        oa, soa, _ = OA.nxt()
                        for qs in range(nqs):
                            P.op(DVE, I("tensor_scalar", oa[:, qs, :], po[:, qs, 0:128], rc[:, qs:qs + 1], None, op0=ALU.mult),
                                 reads=[spo, src], writes=[soa])
                        state[("oa", w["ucol"], q0)] = (oa, soa)
                        return
                    zt, szt, semz = ZT.nxt()
                    P.dma(POOL, zt[:, 0:nqs, :], ZZ[tok, w["ucol"]:w["ucol"] + 128].rearrange("(q p) d -> p q d", p=128), semz, writes=[szt])
                    ut, sut, semu = UT.nxt()
                    if w["bmode"] is None:
                        for qs in range(nqs):
                            P.op(DVE, I("scalar_tensor_tensor", ut[:, qs, :], po[:, qs, 0:128], rc[:, qs:qs + 1], zt[:, qs, :],
                                        op0=ALU.mult, op1=ALU.mult), reads=[spo, src, szt], writes=[sut])
                    else:
                        oa, soa = state.pop(("oa", w["ucol"], q0))
                        ob, sob, _ = OB.nxt()
                        oc, soc, _ = OC.nxt()
                        P.op(DVE, I("tensor_scalar", rc[:, 0:nqs], rc[:, 0:nqs], lamv[:, 0:1], None, op0=ALU.mult), reads=[src, S_lw], writes=[src])
                        for qs in range(nqs):
                            P.op(DVE, I("scalar_tensor_tensor", ob[:, qs, :], po[:, qs, 0:128], rc[:, qs:qs + 1], oa[:, qs, :],
                                        op0=ALU.mult, op1=ALU.add), reads=[spo, src, soa], writes=[sob])
                        P.op(POOL, I("tensor_tensor", oc[:, 0:nqs, :], ob[:, 0:nqs, :], ob[:, 0:nqs, :], ALU.mult), reads=[sob], writes=[soc])
                        P.op(DVE, I("tensor_reduce", rc[:, 4:4 + nqs], oc[:, 0:nqs, :], AX.X, ALU.add), reads=[soc], writes=[src])
                        rstd_ops(rc[:, 4:4 + nqs], nqs, 1.0 / 128, [src, S_const], [src])
                        for qs in range(nqs):
                            P.op(DVE, I("scalar_tensor_tensor", oc[:, qs, :], ob[:, qs, :], rc[:, 4 + qs:5 + qs], g_sub,
                                        op0=ALU.mult, op1=ALU.mult), reads=[sob, src, S_lw], writes=[soc])
                        P.op(POOL, I("tensor_tensor", ut[:, 0:nqs, :], oc[:, 0:nqs, :], zt[:, 0:nqs, :], ALU.mult), reads=[soc, szt], writes=[sut])
                    P.dma(POOL, UU[tok, w["ucol"]:w["ucol"] + 128].rearrange("(q p) d -> p q d", p=128), ut[:, 0:nqs, :], semu, reads=[sut])

                nw = len(work)
                for i in range(nw + 2):
                    if i < nw:
                        qk(work[i])
                        ex(work[i])
                    if i >= 2:
                        pv(work[i - 2])
                work.clear()

            jobs = []

            def mk_gqa(mx, zc, sc_, kv):
                def ld():
                    c = {}
                    c["k"] = load_k(KT[mx][kv], KTB)
                    c["v"] = load_v(mx, kv)
                    c["q"] = [load_k(QT[mx][kv * 2 + g], QB) for g in range(2)]
                    return c

                def wk(c):
                    for g in range(2):
                        hq = kv * 2 + g
                        add_head(mx, [(c["k"][0], c["k"][1], 128)], [(c["q"][g][0], c["q"][g][1], 128)], c["v"][0], c["v"][1], kv,
                                 zc + hq * 128, sc_, hq if mx == "A" else None, None)
                    emit_all()
                return ld, wk

            def mk_mla(h):
                def ld():
                    c = {}
                    if h == 0:
                        kr_state["kr"] = load_k(KT["Cr"][0], KRB)
                    c["k"] = load_k(KT["Cn"][h], KTB)
                    c["v"] = load_v("C", h)
                    c["q"] = load_k(QT["Cn"][h], QB)
                    c["qr"] = load_k(QT["Cr"][h], QRB)
                    return c

                def wk(c):
                    krap, skr = kr_state["kr"]
                    add_head("C", [(c["k"][0], c["k"][1], 128), (krap, skr, 128)], [(c["q"][0], c["q"][1], 128), (c["qr"][0], c["qr"][1], 128)],
                             c["v"][0], c["v"][1], h, 1024 + h * 128, sC, None, None)
                    emit_all()
                return ld, wk

            def mk_diff(h):
                def ld():
                    c = {}
                    c["k"] = load_k(KT["B"][h], KTB)
                    c["v"] = load_v("B", h)
                    qs_ = []
                    for cc in range(2):
                        qap, sq, semq = QB.nxt()
                        z0, z1 = (64, 128) if cc == 0 else (0, 64)
                        d0, d1 = (0, 64) if cc == 0 else (64, 128)
                        P.op(DVE if cc == 0 else POOL, I("memset", qap[z0:z1, :], 0.0), writes=[sq])
                        P.dma(SP, qap[d0:d1, :], QT["B"][h][d0:d1, :], semq, writes=[sq])
                        qs_.append((qap, sq))
                    c["q"] = qs_
                    return c

                def wk(c):
                    for cc in range(2):
                        add_head("B", [(c["k"][0], c["k"][1], 128)], [(c["q"][cc][0], c["q"][cc][1], 128)], c["v"][0], c["v"][1], h,
                                 512 + h * 128, sB, None, cc)
                    work.sort(key=lambda w: (w["q0"], w["bmode"]))
                    emit_all()
                return ld, wk

            kr_state = {}
            for mx, zc, sc_ in (("A", 0, sA), ("D", 1536, sA)):
                for kv in range(2):
                    jobs.append(mk_gqa(mx, zc, sc_, kv))
            for h in range(4):
                jobs.append(mk_mla(h))
            for h in range(4):
                jobs.append(mk_diff(h))
            ctxs = [None] * len(jobs)
            ctxs[0] = jobs[0][0]()
            for ji in range(len(jobs)):
                if ji + 1 < len(jobs):
                    ctxs[ji + 1] = jobs[ji + 1][0]()
                jobs[ji][1](ctxs[ji])
            P.barrier()
            P.flush()
            for b_ in (KTB, KRB, VB, QB, QRB, ZT, UT):
                b_.release()
        if stop_after == "P3":
            break

        with ExitStack() as es:
            wo = sb(es, "wo", [128, 16, D], BF16)
            S_wo = Slot()
            c_fng = sb(es, "c_fng", [128, D], F32)
            fsem = P.getsem()
            P.dma(SP, c_fng, fng.partition_broadcast(128), fsem, writes=[S_const])
            UB = Buf(P, es, nc, "ub", [128, D], BF16, n=3)
            XB = Buf(P, es, nc, "xb4", [128, D], F32, n=3)
            UTT = Buf(P, es, nc, "utt", [128, 16, 128], BF16, n=2, dmasem=False)
            XN = Buf(P, es, nc, "xn", [128, D], F32, n=2, sw=True)
            JK = Buf(P, es, nc, "jk4", [128, D], BF16, n=1, dmasem=False)
            ST = Buf(P, es, nc, "st4", [128, 1], F32, n=2, dmasem=False)
            PTR = Buf(P, es, nc, "ptr4", [128, 8, 128], BF16, n=2, psum=True, dmasem=False)
            PY = Buf(P, es, nc, "py4", [128, 512], F32, n=4, psum=True, dmasem=False)
            wsem = P.getsem()
            for i in range(4):
                P.dma(SP if i % 2 == 0 else ACT, wo[:, :, i * 512:(i + 1) * 512],
                      WOB[:, i * 512:(i + 1) * 512].rearrange("(c p) n -> p c n", p=128), wsem, writes=[S_wo])
            def p4a(tt):
                tsl = slice(tt * 128, (tt + 1) * 128)
                ub, sub_, semu = UB.nxt()
                P.dma(SP, ub, UU[tsl, :], semu, writes=[sub_])
                xt, sx, semx = XB.nxt()
                P.dma(SP, xt, xsrc[tsl, :], semx, writes=[sx])
                utt, sutt, _ = UTT.nxt()
                for h8 in range(2):
                    pt, spt, _ = PTR.nxt()
                    fns = [I("transpose", pt[:, j, :], ub[:, (h8 * 8 + j) * 128:(h8 * 8 + j + 1) * 128], c_identb) for j in range(8)]
                    P.op(PE, fns, reads=[sub_, S_const], writes=[spt])
                    if h8 == 0:
                        P.op(ACT, I("copy", utt[:, 0:8, :], pt), reads=[spt], writes=[sutt])
                    else:
                        P.op(DVE, I("tensor_copy", utt[:, 8:16, :], pt), reads=[spt], writes=[sutt])
                return utt, sutt, xt, sx

            def p4b(tt, utt, sutt, xt, sx):
                v = 1 if tt < NC_T else 0
                tsl = slice(tt * 128, (tt + 1) * 128)
                xn, sxn, semn = XN.nxt()
                for cq in range(4):
                    py, spy, _ = PY.nxt()
                    fns = [I("matmul", py, utt[:, mc, :], wo[:, mc, cq * 512:(cq + 1) * 512], start=(mc == 0), stop=(mc == 15))
                           for mc in range(16)]
                    P.op(PE, fns, reads=[sutt, S_wo], writes=[spy])
                    cs = slice(cq * 512, (cq + 1) * 512)
                    P.op(DVE, I("tensor_tensor", xn[:, cs], py, gate_bc[:, v, cs], ALU.mult), reads=[spy, S_mod], writes=[sxn])
                P.op(POOL, I("tensor_tensor", xn, xn, xt, ALU.add), reads=[sxn, sx], writes=[sxn])
                if not last:
                    P.dma(POOL, XS[tsl, :], xn, semn, reads=[sxn])
                else:
                    jk, sj, _ = JK.nxt()
                    st, sst, _ = ST.nxt()
                    P.op(ACT, I("activation", jk, xn, AF.Square, scale=1.0 / math.sqrt(D), accum_out=st), reads=[sxn], writes=[sj, sst])
                    rstd_ops(st, 1, 1.0, [sst, S_const], [sst])
                    P.op(DVE, I("scalar_tensor_tensor", xn, xn, st, c_fng, op0=ALU.mult, op1=ALU.mult), reads=[sxn, sst, S_const], writes=[sxn])
                    P.dma(POOL, out[(tt - NC_T) * 128:(tt - NC_T + 1) * 128, :], xn, semn, reads=[sxn])

            tl4 = list(range(NC_T if last else 0, NT))
            prev4 = None
            for i4, tt in enumerate(tl4):
                cur4 = p4a(tt)
                if prev4 is not None:
                    p4b(tl4[i4 - 1], *prev4)
                prev4 = cur4
            p4b(tl4[-1], *prev4)
            P.barrier()
            P.flush()
            for b_ in (UB, XB, XN):
                b_.release()
            P.pool.append(fsem)
            P.pool.append(wsem)

    P.barrier()
    P.flush()
    P.close()
    top.close()
    return nc, dbg


def _rope_tables(rot_dim):
    rows = 4096 // 64
    row = np.broadcast_to(np.arange(rows)[:, None], (rows, 64)).reshape(-1).astype(np.float32)
    col = np.broadcast_to(np.arange(64)[None, :], (rows, 64)).reshape(-1).astype(np.float32)
    axis_dim = rot_dim // 2
    inv_freq = (np.float32(10000.0) ** (-np.arange(0, axis_dim, 2, dtype=np.float32) / np.float32(axis_dim))).astype(np.float32)
    ang_r = (row[:, None] * inv_freq[None, :]).astype(np.float32)
    ang_c = (col[:, None] * inv_freq[None, :]).astype(np.float32)
    ang = np.concatenate([ang_r, ang_r, ang_c, ang_c], axis=-1)
    cos = np.cos(ang).astype(np.float32)
    sin = np.sin(ang).astype(np.float32)
    q = rot_dim // 4
    sgn = np.concatenate([-np.ones(q), np.ones(q), -np.ones(q), np.ones(q)]).astype(np.float32)
    return cos, sin * sgn[None, :]


def _consts():
    cH, sH = _rope_tables(128)
    cB, sB = _rope_tables(64)
    ropeH = np.stack([cH, sH]).astype(np.float32)
    ropeB = np.stack([cB, sB]).astype(np.float32)
    identf = np.eye(128, dtype=np.float32)
    identb = np.eye(128, dtype=np.float32).astype(ml_dtypes.bfloat16)
    sel = np.zeros((2, 2, 128), np.float32)
    sel[0, 0, :] = 1.0
    sel[1, 1, :] = 1.0
    am = np.zeros((6, 128, 512), np.float32)
    i = np.arange(128)[:, None]
    qq = np.arange(512)[None, :]
    for m in range(6):
        kpos = (m - 1) * 128 + i
        am[m] = (np.abs(qq - kpos) <= 128)
    return dict(ropeH=ropeH, ropeB=ropeB, identf=identf, identb=identb, sel=sel, amask=am.astype(ml_dtypes.bfloat16))


_CACHE = {}


def _prep_inputs(inp, b, consts):
    f = lambda a: np.ascontiguousarray(np.asarray(a, dtype=np.float32))
    m = dict(consts)
    m["xin"] = np.ascontiguousarray(np.concatenate([inp["ctx"][b], inp["x"][b]], axis=0).astype(np.float32))
    cv = np.stack([inp["c"][b], inp["c_ctx"]], axis=0).astype(np.float32)
    m["cT"] = np.ascontiguousarray(cv.reshape(2, 16, 128).transpose(2, 1, 0))
    m["w_mod"] = f(inp["w_mod"])
    bm = np.asarray(inp["b_mod"], np.float32)
    m["bmod_row"] = np.ascontiguousarray(bm.reshape(DEPTH, 1, 3 * D))
    m["normg_col"] = np.ascontiguousarray(np.asarray(inp["norm_g"], np.float32).reshape(DEPTH, 16, 128).transpose(0, 2, 1))
    m["w_in"] = f(inp["w_in"])
    m["cqg"] = f(inp["c_q_norm_g"]).reshape(DEPTH, 1, 448)
    m["ckvg"] = f(inp["c_kv_norm_g"]).reshape(DEPTH, 1, 128)
    m["w_uq"] = f(inp["c_w_uq"])
    m["w_ukv"] = f(inp["c_w_ukv"])
    m["dqg"] = f(inp["d_q_norm_g"]).reshape(DEPTH, 1, 128)
    m["dkg"] = f(inp["d_k_norm_g"]).reshape(DEPTH, 1, 128)
    m["a_sink"] = f(inp["a_sink"]).reshape(DEPTH, 1, 4)
    m["b_lambda"] = f(inp["b_lambda"]).reshape(DEPTH, 1, 256)
    m["sublng"] = f(inp["b_subln_g"]).reshape(DEPTH, 1, 128)
    m["w_out"] = f(inp["w_out"])
    m["fng"] = f(inp["final_norm_g"]).reshape(1, D)
    return m


def kernel(**inputs):
    inp = {k: np.asarray(v) for k, v in inputs.items()}
    if "nc" not in _CACHE:
        _CACHE["nc"] = build_program()[0]
    nc = _CACHE["nc"]
    consts = _consts()
    in_maps = [_prep_inputs(inp, b, consts) for b in range(8)]
    res = run_bass_kernel_spmd(nc, in_maps, core_ids=list(range(8)))
    return np.stack([np.asarray(r["out"], dtype=np.float32) for r in res.results], axis=0)
```

```python
import math
from contextlib import ExitStack
import numpy as np
import ml_dtypes
import concourse.bass as bass
import concourse.mybir as mybir
from concourse.bass_utils import run_bass_kernel_spmd

F32 = mybir.dt.float32
BF16 = mybir.dt.bfloat16
AF = mybir.ActivationFunctionType
ALU = mybir.AluOpType
AX = mybir.AxisListType
PE, ACT, DVE, POOL, SP = "tensor", "scalar", "vector", "gpsimd", "sync"
ENGS = (PE, ACT, DVE, POOL, SP)

D = 2048
T = 4352
NT = 34
NC_T = 2
DEPTH = 2
INW = 6272
EPS = 1e-6
GROUPS = [(0, 512), (512, 512), (1024, 512), (1536, 512), (2048, 512), (2560, 512), (3072, 512),
          (3584, 448), (4032, 192), (4224, 512), (4736, 512), (5248, 512), (5760, 512)]


_UID = [0]
P2_ONLY = None
G8SKIP = None


def I(method, *args, **kwargs):
    return (method, args, kwargs)


class Slot:
    __slots__ = ("w", "r")

    def __init__(self):
        self.w = None
        self.r = []


class Prog:
    def __init__(self, nc):
        self.nc = nc
        self.streams = {e: [] for e in ENGS}
        self.sems = {}
        self.cnt = {}
        self.seen = {e: {} for e in ENGS}
        self._stack = []
        self.pool = []
        self.pool_sw = []
        self.kind = {}
        for e in (PE, ACT, DVE, POOL):
            self.newsem("E_" + e)

    def newsem(self, name):
        cm = self.nc.semaphore(name)
        s = cm.__enter__()
        self._stack.append(cm)
        self.sems[name] = s
        self.cnt[name] = 0
        return name

    def getsem(self, sw=False):
        pool = self.pool_sw if sw else self.pool
        if pool:
            return pool.pop()
        name = self.newsem(("dqs%d" if sw else "dq%d") % len(self.sems))
        self.kind[name] = sw
        return name

    def _emit_waits(self, eng, toks):
        need = {}
        for t in toks:
            if t is None:
                continue
            s, v = t
            if self.seen[eng].get(s, 0) >= v:
                continue
            if need.get(s, 0) < v:
                need[s] = v
        for s, v in need.items():
            self.seen[eng][s] = v
            self.streams[eng].append(("wait", s, v))

    def _deps(self, reads, writes, extra):
        toks = list(extra)
        for s in reads:
            toks.append(s.w)
        for s in writes:
            toks.append(s.w)
            toks.extend(s.r)
        return toks

    def _update(self, tok, reads, writes):
        for s in reads:
            s.r.append(tok)
        for s in writes:
            s.w = tok
            s.r = []

    def op(self, eng, fns, reads=(), writes=(), extra=()):
        if isinstance(fns, tuple):
            fns = [fns]
        self._emit_waits(eng, self._deps(reads, writes, extra))
        sname = "E_" + eng
        self.cnt[sname] += 1
        tok = (sname, self.cnt[sname])
        for f in fns[:-1]:
            self.streams[eng].append(("ins", f, None))
        self.streams[eng].append(("ins", fns[-1], (sname, 1)))
        self._update(tok, reads, writes)
        return tok

    def dma(self, q, out, in_, sem, reads=(), writes=(), extra=(), **kw):
        assert self.kind[sem] == (q == POOL), (sem, q)
        self._emit_waits(q, self._deps(reads, writes, extra))
        self.cnt[sem] += 16
        tok = (sem, self.cnt[sem])
        self.streams[q].append(("ins", ("dma_start", (), dict(out=out, in_=in_, **kw)), (sem, 16)))
        self._update(tok, reads, writes)
        return tok

    def barrier(self):
        toks = [(s, c) for s, c in self.cnt.items() if c > 0]
        for e in ENGS:
            self._emit_waits(e, toks)

    def flush(self):
        nc = self.nc
        streams, sems = self.streams, self.sems

        def run(eng, e):
            for it in streams[eng]:
                if it[0] == "wait":
                    e.wait_ge(sems[it[1]], it[2])
                else:
                    f = it[1]
                    ins = getattr(e, f[0])(*f[1], **f[2])
                    if it[2] is not None:
                        ins.then_inc(sems[it[2][0]], it[2][1])

        with nc.Block() as block:
            @block.sync
            def _(e):
                run(SP, e)

            @block.tensor
            def _(e):
                run(PE, e)

            @block.scalar
            def _(e):
                run(ACT, e)

            @block.vector
            def _(e):
                run(DVE, e)

            @block.gpsimd
            def _(e):
                run(POOL, e)
        self.streams = {e: [] for e in ENGS}

    def close(self):
        for cm in reversed(self._stack):
            cm.__exit__(None, None, None)


class Buf:
    def __init__(self, P, es, nc, name, shape, dt, n=1, psum=False, dmasem=True, sw=False):
        self.aps, self.slots, self.sems = [], [], []
        for i in range(n):
            _UID[0] += 1
            if psum:
                t = es.enter_context(nc.psum_tensor("%s%d_%d" % (name, i, _UID[0]), shape, dt))
            else:
                t = es.enter_context(nc.sbuf_tensor("%s%d_%d" % (name, i, _UID[0]), shape, dt))
            self.aps.append(t.ap())
            self.slots.append(Slot())
            self.sems.append(P.getsem(sw) if dmasem else None)
        self.n = n
        self.i = -1
        self.P = P
        self.sw = sw

    def nxt(self):
        self.i = (self.i + 1) % self.n
        return self.aps[self.i], self.slots[self.i], self.sems[self.i]

    def release(self):
        for s in self.sems:
            if s is not None:
                (self.P.pool_sw if self.sw else self.P.pool).append(s)


def build_program(debug=False, nlayers=DEPTH, stop_after=None):
    nc = bass.Bass("TRN2", target_bir_lowering=False)
    dbg = {}

    def din(name, shape, dt=F32):
        return nc.dram_tensor(name, list(shape), dt, kind="ExternalInput").ap()

    def dscr(name, shape, dt):
        if debug and (debug is True or name in debug):
            a = nc.dram_tensor(name, list(shape), dt, kind="ExternalOutput").ap()
            dbg[name] = a
            return a
        return nc.dram_tensor(name, list(shape), dt, kind="Internal").ap()

    xin = din("xin", [T, D])
    cT = din("cT", [128, 16, 2])
    w_mod = din("w_mod", [DEPTH, D, 3 * D])
    bmod_row = din("bmod_row", [DEPTH, 1, 3 * D])
    normg_col = din("normg_col", [DEPTH, 128, 16])
    w_in = din("w_in", [DEPTH, D, INW])
    cqg = din("cqg", [DEPTH, 1, 448])
    ckvg = din("ckvg", [DEPTH, 1, 128])
    w_uq = din("w_uq", [DEPTH, 448, 768])
    w_ukv = din("w_ukv", [DEPTH, 128, 1024])
    dqg = din("dqg", [DEPTH, 1, 128])
    dkg = din("dkg", [DEPTH, 1, 128])
    a_sink = din("a_sink", [DEPTH, 1, 4])
    b_lambda = din("b_lambda", [DEPTH, 1, 256])
    sublng = din("sublng", [DEPTH, 1, 128])
    w_out = din("w_out", [DEPTH, D, D])
    fng = din("fng", [1, D])
    ropeH = din("ropeH", [2, 4096, 128])
    ropeB = din("ropeB", [2, 4096, 64])
    identf = din("identf", [128, 128])
    identb = din("identb", [128, 128], BF16)
    sel = din("sel", [2, 2, 128])
    amask = din("amask", [6, 128, 512], BF16)
    out = nc.dram_tensor("out", [4096, D], F32, kind="ExternalOutput").ap()

    WB = dscr("WB", [D, INW], BF16)
    QT = {"A": dscr("QT_A", [4, 128, T], BF16), "B": dscr("QT_B", [4, 128, T], BF16),
          "Cn": dscr("QT_Cn", [4, 128, T], BF16), "Cr": dscr("QT_Cr", [4, 64, T], BF16),
          "D": dscr("QT_D", [4, 128, T], BF16)}
    KT = {"A": dscr("KT_A", [2, 128, T], BF16), "B": dscr("KT_B", [4, 128, T], BF16),
          "Cn": dscr("KT_Cn", [4, 128, T], BF16), "Cr": dscr("KT_Cr", [1, 64, T], BF16),
          "D": dscr("KT_D", [2, 128, T], BF16)}
    VV = {"A": dscr("V_A", [T, 256], BF16), "B": dscr("V_B", [T, 512], BF16),
          "C": dscr("V_C", [T, 512], BF16), "D": dscr("V_D", [T, 256], BF16)}
    WOB = dscr("WOB", [D, D], BF16)
    ZZ = dscr("ZZ", [T, D], BF16)
    UU = dscr("UU", [T, D], BF16)
    XS = dscr("XS", [T, D], F32)
    if debug:
        HT_dbg = dscr("HT_dbg", [128, 16, T], BF16)
        MOD_dbg = dscr("MOD_dbg", [128, 64 + 2 * D], F32)
    dbg_ht = bool(debug) and (debug is True or "HT_dbg" in debug)
    dbg_mod = bool(debug) and (debug is True or "MOD_dbg" in debug)

    P = Prog(nc)
    top = ExitStack()

    def sb(es, name, shape, dt):
        _UID[0] += 1
        return es.enter_context(nc.sbuf_tensor("%s_%d" % (name, _UID[0]), list(shape), dt)).ap()

    c_identf = sb(top, "c_identf", [128, 128], F32)
    c_identb = sb(top, "c_identb", [128, 128], BF16)
    c_sel = sb(top, "c_sel", [2, 2, 128], F32)
    c_amask = sb(top, "c_amask", [128, 6, 512], BF16)
    modA = sb(top, "modA", [128, 2, 16], F32)
    modB = sb(top, "modB", [128, 2, 16], F32)
    gate_bc = sb(top, "gate_bc", [128, 2, D], F32)
    g_cq = sb(top, "g_cq", [128, 448], F32)
    g_ckv = sb(top, "g_ckv", [128, 128], F32)
    g_dq = sb(top, "g_dq", [128, 128], F32)
    g_dk = sb(top, "g_dk", [128, 128], F32)
    g_sub = sb(top, "g_sub", [128, 128], F32)
    esink = sb(top, "esink", [128, 4], F32)
    lamv = sb(top, "lamv", [128, 4], F32)
    wuq_b = sb(top, "wuq_b", [128, 4, 768], BF16)
    wukv_b = sb(top, "wukv_b", [128, 1024], BF16)
    S_const = Slot()
    S_mod = Slot()
    S_lw = Slot()
    csem = P.getsem()
    P.dma(SP, c_identf, identf, csem, writes=[S_const])
    P.dma(SP, c_identb, identb, csem, writes=[S_const])
    P.dma(SP, c_sel, sel, csem, writes=[S_const])
    P.dma(SP, c_amask, amask.rearrange("m p q -> p m q"), csem, writes=[S_const])
    P.barrier()
    P.flush()

    def rstd_ops(ss, n, scale, rd, wr):
        P.op(ACT, I("activation", ss, ss, AF.Sqrt, bias=c_eps, scale=scale), reads=rd, writes=wr)
        P.op(DVE, I("reciprocal", ss, ss), reads=wr, writes=wr)

    c_eps = sb(top, "c_eps", [128, 1], F32)
    P.op(DVE, I("memset", c_eps, EPS), writes=[S_const])

    for l in range(nlayers):
        last = (l == DEPTH - 1)
        lam_init = 0.8 - 0.6 * math.exp(-0.3 * l)
        xsrc = xin if l == 0 else XS

        with ExitStack() as es:
            sc = sb(es, "sc", [128, 16, 2], F32)
            bcol = sb(es, "bcol", [128, 32], F32)
            gcol = sb(es, "gcol", [128, 16], F32)
            modc = sb(es, "modc", [128, 32, 2], F32)
            mrow = sb(es, "mrow", [2, 3 * D], F32)
            brow = sb(es, "brow", [2, 3 * D], F32)
            lamb = sb(es, "lamb", [128, 256], F32)
            lamt = sb(es, "lamt", [128, 256], F32)
            wst = sb(es, "wst", [128, 4, 768], F32)
            wst2 = sb(es, "wst2", [128, 1024], F32)
            WM = Buf(P, es, nc, "wm", [128, 16, 512], F32, n=2)
            ps_col = Buf(P, es, nc, "ps_col", [128, 32, 2], F32, n=1, psum=True, dmasem=False)
            ps_row = Buf(P, es, nc, "ps_row", [2, 512], F32, n=2, psum=True, dmasem=False)
            ps_bc = Buf(P, es, nc, "ps_bc", [128, 512], F32, n=2, psum=True, dmasem=False)
            S_sc, S_b, S_row, S_st = Slot(), Slot(), Slot(), Slot()
            sm = P.getsem()
            P.dma(SP, sc, cT, sm, writes=[S_sc])
            P.dma(SP, gcol, normg_col[l], sm, writes=[S_b])
            P.dma(SP, brow, bmod_row[l].partition_broadcast(2), sm, writes=[S_b])
            P.dma(SP, lamb, b_lambda[l].partition_broadcast(128), sm, writes=[S_b])
            P.dma(SP, esink, a_sink[l].partition_broadcast(128), sm, writes=[S_lw])
            P.dma(SP, g_cq, cqg[l].partition_broadcast(128), sm, writes=[S_lw])
            P.dma(SP, g_ckv, ckvg[l].partition_broadcast(128), sm, writes=[S_lw])
            P.dma(SP, g_dq, dqg[l].partition_broadcast(128), sm, writes=[S_lw])
            P.dma(SP, g_dk, dkg[l].partition_broadcast(128), sm, writes=[S_lw])
            P.dma(SP, g_sub, sublng[l].partition_broadcast(128), sm, writes=[S_lw])
            P.dma(SP, wst[:, 0:3, :], w_uq[l, 0:384, :].rearrange("(c p) n -> p c n", p=128), sm, writes=[S_st])
            P.dma(SP, wst[0:64, 3, :], w_uq[l, 384:448, :], sm, writes=[S_st])
            P.dma(SP, wst2, w_ukv[l], sm, writes=[S_st])
            tokall = (sm, P.cnt[sm])
            for s_ in (S_sc, S_b, S_lw, S_st):
                s_.w = tokall
            P.op(DVE, I("tensor_copy", wuq_b[:, 0:3, :], wst[:, 0:3, :]), reads=[S_st], writes=[S_lw])
            P.op(DVE, I("tensor_copy", wuq_b[0:64, 3, :], wst[0:64, 3, :]), reads=[S_st], writes=[S_lw])
            P.op(DVE, I("tensor_copy", wukv_b, wst2), reads=[S_st], writes=[S_lw])
            P.op(ACT, I("activation", sc, sc, AF.Silu), reads=[S_sc], writes=[S_sc])
            P.op(ACT, I("activation", esink, esink, AF.Exp), reads=[S_lw], writes=[S_lw])
            P.op(DVE, I("tensor_scalar", g_sub, g_sub, 1.0 - lam_init, None, op0=ALU.mult), reads=[S_lw], writes=[S_lw])
            lv = lamb.rearrange("p (a b) -> p a b", a=4)
            lt = lamt.rearrange("p (a b) -> p a b", a=4)
            P.op(DVE, I("tensor_tensor", lt[:, 0, :], lv[:, 0, :], lv[:, 1, :], ALU.mult), reads=[S_b], writes=[S_row])
            P.op(DVE, I("tensor_tensor", lt[:, 1, :], lv[:, 2, :], lv[:, 3, :], ALU.mult), reads=[S_b], writes=[S_row])
            P.op(DVE, I("tensor_reduce", lamv[:, 1:3], lt[:, 0:2, :], AX.X, ALU.add), reads=[S_row], writes=[S_lw])
            P.op(ACT, I("activation", lamv[:, 1:3], lamv[:, 1:3], AF.Exp), reads=[S_lw], writes=[S_lw])
            P.op(DVE, I("tensor_tensor", lamv[:, 3:4], lamv[:, 2:3], lamv[:, 1:2], ALU.subtract), reads=[S_lw], writes=[S_lw])
            P.op(DVE, I("tensor_scalar", lamv[:, 0:1], lamv[:, 3:4], -lam_init, None, op0=ALU.add), reads=[S_lw], writes=[S_lw])
            for cg in range(12):
                wt, sw, semw = WM.nxt()
                P.dma(SP if cg % 2 == 0 else ACT, wt, w_mod[l, :, cg * 512:(cg + 1) * 512].rearrange("(c p) n -> p c n", p=128), semw, writes=[sw])
                pr, spr, _ = ps_row.nxt()
                fns = [I("matmul", pr, sc[:, kc, :], wt[:, kc, :], start=(kc == 0), stop=(kc == 15)) for kc in range(16)]
                P.op(PE, fns, reads=[sw, S_sc], writes=[spr])
                c0 = cg * 512
                P.op(DVE, I("tensor_tensor", mrow[:, c0:c0 + 512], pr, brow[:, c0:c0 + 512], ALU.add),
                     reads=[spr, S_b], writes=[S_row])
            pcol, scol, _ = ps_col.nxt()
            fns = [I("transpose", pcol[:, ch, :], mrow[0:2, ch * 128:(ch + 1) * 128], c_identf[0:2, 0:2]) for ch in range(32)]
            P.op(PE, fns, reads=[S_row, S_const], writes=[scol])
            for v in range(2):
                P.op(DVE, I("scalar_tensor_tensor", modA[:, v, :], pcol[:, 16:32, v], 1.0, gcol, op0=ALU.add, op1=ALU.mult),
                     reads=[scol, S_b], writes=[S_mod])
                P.op(DVE, I("tensor_copy", modB[:, v, :], pcol[:, 0:16, v]), reads=[scol], writes=[S_mod])
            grow = mrow[:, 4096:6144]
            for v in range(2):
                for cq in range(4):
                    pb, spb, _ = ps_bc.nxt()
                    P.op(PE, I("matmul", pb, c_sel[:, v, :], grow[:, cq * 512:(cq + 1) * 512], start=True, stop=True),
                         reads=[S_row, S_const], writes=[spb])
                    P.op(ACT, I("copy", gate_bc[:, v, cq * 512:(cq + 1) * 512], pb), reads=[spb], writes=[S_mod])
            if dbg_mod and l == 0:
                sd = Slot()
                P.dma(SP, MOD_dbg[:, 0:32], modA.rearrange("p a b -> p (a b)"), sm, reads=[S_mod], writes=[sd])
                P.dma(SP, MOD_dbg[:, 32:64], modB.rearrange("p a b -> p (a b)"), sm, reads=[S_mod], writes=[sd])
                P.dma(SP, MOD_dbg[:, 64:64 + 2 * D], gate_bc.rearrange("p a b -> p (a b)"), sm, reads=[S_mod], writes=[sd])
            P.barrier()
            P.flush()
            WM.release()
            P.pool.append(sm)
        if stop_after == "P0":
            break

        esr = ExitStack()
        c_ropeH = sb(esr, "c_ropeH", [128, 2, 32, 128], F32)
        c_ropeB = sb(esr, "c_ropeB", [128, 2, 32, 64], F32)
        rsem = P.getsem()
        P.dma(SP, c_ropeH, ropeH.rearrange("c (t p) d -> p c t d", p=128), rsem, writes=[S_const])
        P.dma(SP, c_ropeB, ropeB.rearrange("c (t p) d -> p c t d", p=128), rsem, writes=[S_const])
        P.barrier()
        P.pool.append(rsem)
        PASSES = [list(range(0, 9)), list(range(9, 18)), list(range(18, 26)), list(range(26, 34))]
        for hf, tiles in enumerate(PASSES):
            with ExitStack() as es:
                hT = sb(es, "hT", [128, 16, 9 * 128], BF16)
                S_h = [Slot() for _ in range(9)]
                with ExitStack() as es1:
                    XB = Buf(P, es1, nc, "xb", [128, D], F32, n=3)
                    XSC = Buf(P, es1, nc, "xsc", [128, D], F32, n=2, dmasem=False)
                    JK = Buf(P, es1, nc, "jk", [128, D], BF16, n=2, dmasem=False)
                    ST = Buf(P, es1, nc, "st1", [128, 1], F32, n=3, dmasem=False)
                    PT = Buf(P, es1, nc, "pt1", [128, 4, 128], F32, n=4, psum=True, dmasem=False)
                    def p1a(ti, tt):
                        xt, sx, semx = XB.nxt()
                        P.dma(SP, xt, xsrc[tt * 128:(tt + 1) * 128, :], semx, writes=[sx])
                        jk, sj, _ = JK.nxt()
                        st, sst, _ = ST.nxt()
                        P.op(ACT, I("activation", jk, xt, AF.Square, scale=1.0 / math.sqrt(D), accum_out=st),
                             reads=[sx], writes=[sj, sst])
                        rstd_ops(st, 1, 1.0, [sst, S_const], [sst])
                        xs, sxs, _ = XSC.nxt()
                        P.op(DVE, I("tensor_scalar", xs, xt, st, None, op0=ALU.mult), reads=[sx, sst], writes=[sxs])
                        return xs, sxs

                    def p1b(ti, tt, xs, sxs):
                        v = 1 if tt < NC_T else 0
                        for q4 in range(4):
                            pt, spt, _ = PT.nxt()
                            fns = [I("transpose", pt[:, j, :], xs[:, (q4 * 4 + j) * 128:(q4 * 4 + j + 1) * 128], c_identf)
                                   for j in range(4)]
                            P.op(PE, fns, reads=[sxs, S_const], writes=[spt])
                            for j in range(4):
                                ch = q4 * 4 + j
                                dst = hT[:, ch, ti * 128:(ti + 1) * 128]
                                if j % 2 == 0:
                                    P.op(DVE, I("tensor_scalar", dst, pt[:, j, :], modA[:, v, ch:ch + 1], modB[:, v, ch:ch + 1],
                                                op0=ALU.mult, op1=ALU.add), reads=[spt, S_mod], writes=[S_h[ti]])
                                else:
                                    P.op(ACT, I("activation", dst, pt[:, j, :], AF.Identity, bias=modB[:, v, ch:ch + 1],
                                                scale=modA[:, v, ch:ch + 1]), reads=[spt, S_mod], writes=[S_h[ti]])

                    prevA = None
                    for ti, tt in enumerate(tiles):
                        curA = p1a(ti, tt)
                        if prevA is not None:
                            p1b(ti - 1, tiles[ti - 1], *prevA)
                        prevA = curA
                    p1b(len(tiles) - 1, tiles[-1], *prevA)
                    if dbg_ht and l == 0:
                        sd = Slot()
                        P.dma(SP, HT_dbg[:, :, tiles[0] * 128:(tiles[-1] + 1) * 128], hT[:, :, 0:len(tiles) * 128], P.getsem(), reads=S_h, writes=[sd])
                    P.barrier()
                    P.flush()
                    XB.release()
                if stop_after == "P1":
                    continue
                with ExitStack() as es2:
                    WBF = Buf(P, es2, nc, "wbf", [128, 16, 512], BF16, n=2)
                    WF = Buf(P, es2, nc, "wf", [128, 16, 256], F32, n=1) if hf == 0 else None
                    WOS = Buf(P, es2, nc, "wos", [128, 16, 256], BF16, n=1) if hf == 0 else None

                    wo_state = {}

                    def wout_load(i):
                        wf, swf, semf = WF.nxt()
                        P.dma(SP, wf, w_out[l, :, i * 256:(i + 1) * 256].rearrange("(c p) n -> p c n", p=128), semf, writes=[swf])
                        wo_state[i] = (wf, swf)

                    def wout_conv(i):
                        wf, swf = wo_state.pop(i)
                        ws_, sws_, semws_ = WOS.nxt()
                        P.op(DVE, I("tensor_copy", ws_, wf), reads=[swf], writes=[sws_])
                        P.dma(SP, WOB[:, i * 256:(i + 1) * 256].rearrange("(c p) n -> p c n", p=128), ws_, semws_, reads=[sws_])
                    PY = Buf(P, es2, nc, "py", [128, 512], F32, n=2, psum=True, dmasem=False)
                    PY2 = Buf(P, es2, nc, "py2", [128, 1024], F32, n=1, psum=True, dmasem=False)
                    PTR = Buf(P, es2, nc, "ptr", [128, 8, 128], BF16, n=2, psum=True, dmasem=False)
                    Y = Buf(P, es2, nc, "y", [128, 512], F32, n=2, dmasem=False)
                    Y2 = Buf(P, es2, nc, "y2", [128, 1024], F32, n=1, dmasem=False)
                    T1 = Buf(P, es2, nc, "t1", [128, 512], F32, n=2, dmasem=False)
                    T2 = Buf(P, es2, nc, "t2", [128, 512], F32, n=2, dmasem=False)
                    YB = Buf(P, es2, nc, "yb", [128, 512], BF16, n=6, sw=True)
                    STG = Buf(P, es2, nc, "stg", [128, 4, 128], BF16, n=4, sw=True)
                    CT = Buf(P, es2, nc, "ct", [128, 4, 128], BF16, n=2, dmasem=False)
                    SS = Buf(P, es2, nc, "ss", [128, 4], F32, n=2, dmasem=False)

                    def evac(dst, src, rd, wr, eng=None):
                        if eng == ACT:
                            return P.op(ACT, I("copy", dst, src), reads=rd, writes=wr)
                        return P.op(DVE, I("tensor_copy", dst, src), reads=rd, writes=wr)

                    def rope(y, sy, W, tab, q, tt, outb, sob):
                        lt = tt - NC_T
                        wd = min(W, tab.shape[-1])
                        nb = W // wd
                        cosv = tab[:, 0, lt, 0:wd]
                        sinv = tab[:, 1, lt, 0:wd]
                        t1, st1, _ = T1.nxt()
                        t2, st2, _ = T2.nxt()
                        yv = y[:, 0:W].rearrange("p (h d) -> p h d", d=wd)
                        t1v = t1[:, 0:W].rearrange("p (h d) -> p h d", d=wd)
                        t2v = t2[:, 0:W].rearrange("p (h d) -> p h d", d=wd)
                        cb = cosv.unsqueeze(1).broadcast_to([128, nb, wd])
                        P.op(DVE, I("tensor_tensor", t1v, yv, cb, ALU.mult), reads=[sy, S_const], writes=[st1])
                        yq = y[:, 0:W].rearrange("p (h g b q) -> p h g b q", h=nb, b=2, q=q)
                        tq = t2[:, 0:W].rearrange("p (h g b q) -> p h g b q", h=nb, b=2, q=q)
                        sq = sinv.rearrange("p (g b q) -> p g b q", b=2, q=q)
                        for b in range(2):
                            fn = I("tensor_tensor", tq[:, :, :, b, :], yq[:, :, :, 1 - b, :],
                                   sq[:, :, b, :].unsqueeze(1).broadcast_to([128, nb, wd // (2 * q), q]), ALU.mult)
                            P.op(DVE, fn, reads=[sy, S_const], writes=[st2])
                        P.op(DVE, I("tensor_tensor", outb[:, 0:W], t1[:, 0:W], t2[:, 0:W], ALU.add), reads=[st1, st2], writes=[sob])

                    def to_T(yb, syb, blocks, tt):
                        for i0 in range(0, len(blocks), 4):
                            blk = blocks[i0:i0 + 4]
                            pt, spt, _ = PTR.nxt()
                            fns = [I("transpose", pt[:, j, :], yb[:, c0:c0 + 128], c_identb) for j, (c0, _d) in enumerate(blk)]
                            P.op(PE, fns, reads=[syb, S_const], writes=[spt])
                            sg, ssg, semg = STG.nxt()
                            P.op(ACT, I("copy", sg[:, 0:len(blk), :], pt[:, 0:len(blk), :]), reads=[spt], writes=[ssg])
                            for j, (c0, dsts) in enumerate(blk):
                                for (r0, nr, dst) in dsts:
                                    P.dma(POOL, dst[:, tt * 128:(tt + 1) * 128], sg[r0:r0 + nr, j, :], semg, reads=[ssg])

                    def store_tok(src, ssrc, sem, dst):
                        P.dma(POOL, dst, src, sem, reads=[ssrc])

                    def head_norm(y, sy, nh, g):
                        t1, st1, _ = T1.nxt()
                        ss, sss, _ = SS.nxt()
                        W = nh * 128
                        P.op(ACT, [I("activation", t1[:, h * 128:(h + 1) * 128], y[:, h * 128:(h + 1) * 128], AF.Square,
                                     accum_out=ss[:, h:h + 1]) for h in range(nh)], reads=[sy], writes=[st1, sss])
                        rstd_ops(ss[:, 0:nh], nh, 1.0 / 128, [sss, S_const], [sss])
                        for h in range(nh):
                            P.op(DVE, I("scalar_tensor_tensor", y[:, h * 128:(h + 1) * 128], y[:, h * 128:(h + 1) * 128],
                                        ss[:, h:h + 1], g, op0=ALU.mult, op1=ALU.mult), reads=[sss, sy, S_lw], writes=[sy])

                    def post(gi, tt, py, spy):
                        lat = tt >= NC_T
                        tsl = slice(tt * 128, (tt + 1) * 128)
                        if gi in (0, 3, 4, 10):
                            name = {0: ("A", QT), 3: ("B", QT), 4: ("B", KT), 10: ("D", QT)}[gi]
                            dstT = name[1][name[0]]
                            yb, syb, semyb = YB.nxt()
                            if gi == 10 or lat:
                                y, sy, _ = Y.nxt()
                                evac(y, py, [spy], [sy], ACT)
                                if gi == 10:
                                    head_norm(y, sy, 4, g_dq)
                                if lat:
                                    rope(y, sy, 512, c_ropeB if gi in (3, 4) else c_ropeH, 16 if gi in (3, 4) else 32, tt, yb, syb)
                                else:
                                    evac(yb, y, [sy], [syb])
                            else:
                                evac(yb, py, [spy], [syb], ACT)
                            yield
                            yield
                            if gi == 10:
                                yield
                            to_T(yb, syb, [(h * 128, [(0, 128, dstT[h])]) for h in range(4)], tt)
                        elif gi in (1, 11):
                            mx = "A" if gi == 1 else "D"
                            yb, syb, semyb = YB.nxt()
                            if gi == 11 or lat:
                                y, sy, _ = Y.nxt()
                                evac(y[:, 0:256], py[:, 0:256], [spy], [sy], ACT)
                                if gi == 11:
                                    head_norm(y, sy, 2, g_dk)
                                if lat:
                                    rope(y, sy, 256, c_ropeH, 32, tt, yb, syb)
                                else:
                                    evac(yb[:, 0:256], y[:, 0:256], [sy], [syb])
                            else:
                                evac(yb[:, 0:256], py[:, 0:256], [spy], [syb], ACT)
                            P.op(DVE, I("tensor_copy", yb[:, 256:512], py[:, 256:512]), reads=[spy], writes=[syb])
                            yield
                            yield
                            if gi == 11:
                                yield
                            to_T(yb, syb, [(h * 128, [(0, 128, KT[mx][h])]) for h in range(2)], tt)
                            store_tok(yb[:, 256:512], syb, semyb, VV[mx][tsl, :])
                        elif gi == 5:
                            yb, syb, semyb = YB.nxt()
                            evac(yb, py, [spy], [syb], ACT)
                            store_tok(yb, syb, semyb, VV["B"][tsl, :])
                        elif gi in (2, 6, 9, 12):
                            zc = {2: 0, 6: 512, 9: 1024, 12: 1536}[gi]
                            yb, syb, semyb = YB.nxt()
                            P.op(ACT, I("activation", yb, py, AF.Silu), reads=[spy], writes=[syb])
                            store_tok(yb, syb, semyb, ZZ[tsl, zc:zc + 512])
                        elif gi == 7:
                            y, sy, _ = Y.nxt()
                            evac(y[:, 0:448], py[:, 0:448], [spy], [sy], ACT)
                            t1, st1, _ = T1.nxt()
                            ss, sss, _ = SS.nxt()
                            P.op(DVE, I("tensor_tensor", t1[:, 0:448], y[:, 0:448], y[:, 0:448], ALU.mult), reads=[sy], writes=[st1])
                            P.op(DVE, I("tensor_reduce", ss[:, 0:1], t1[:, 0:448], AX.X, ALU.add), reads=[st1], writes=[sss])
                            rstd_ops(ss[:, 0:1], 1, 1.0 / 448, [sss, S_const], [sss])
                            yb, syb, semyb = YB.nxt()
                            P.op(DVE, I("scalar_tensor_tensor", yb[:, 0:448], y[:, 0:448], ss[:, 0:1], g_cq, op0=ALU.mult, op1=ALU.mult),
                                 reads=[sy, sss, S_lw], writes=[syb])
                            yield
                            pt, spt, _ = PTR.nxt()
                            fns = [I("transpose", pt[:, j, :], yb[:, j * 128:(j + 1) * 128], c_identb) for j in range(4)]
                            P.op(PE, fns, reads=[syb, S_const], writes=[spt])
                            ct, sct, _ = CT.nxt()
                            P.op(ACT, I("copy", ct[:, 0:3, :], pt[:, 0:3, :]), reads=[spt], writes=[sct])
                            P.op(ACT, I("copy", ct[0:64, 3, :], pt[0:64, 3, :]), reads=[spt], writes=[sct])
                            yield
                            p2, sp2, _ = PY2.nxt()
                            fns = []
                            for (c0, cw) in ((0, 512), (512, 256)):
                                for j in range(4):
                                    kk = 128 if j < 3 else 64
                                    fns.append(I("matmul", p2[:, c0:c0 + cw], ct[0:kk, j, :], wuq_b[0:kk, j, c0:c0 + cw],
                                                 start=(j == 0), stop=(j == 3)))
                            P.op(PE, fns, reads=[sct, S_lw], writes=[sp2])
                            y2, sy2, _ = Y2.nxt()
                            evac(y2[:, 0:512], p2[:, 0:512], [sp2], [sy2], ACT)
                            evac(y2[:, 512:768], p2[:, 512:768], [sp2], [sy2])
                            y2v = y2[:, 0:768].rearrange("p (h d) -> p h d", d=192)
                            ybn, sybn, _ = YB.nxt()
                            P.op(DVE, I("tensor_copy", ybn.rearrange("p (h d) -> p h d", d=128), y2v[:, :, 0:128]), reads=[sy2], writes=[sybn])
                            ybr, sybr, _ = YB.nxt()
                            if lat:
                                y, sy, _ = Y.nxt()
                                P.op(DVE, I("tensor_copy", y[:, 0:256].rearrange("p (h d) -> p h d", d=64), y2v[:, :, 128:192]),
                                     reads=[sy2], writes=[sy])
                                rope(y, sy, 256, c_ropeB, 16, tt, ybr, sybr)
                            else:
                                P.op(DVE, I("tensor_copy", ybr[:, 0:256].rearrange("p (h d) -> p h d", d=64), y2v[:, :, 128:192]),
                                     reads=[sy2], writes=[sybr])
                            yield
                            to_T(ybn, sybn, [(h * 128, [(0, 128, QT["Cn"][h])]) for h in range(4)], tt)
                            to_T(ybr, sybr, [(0, [(0, 64, QT["Cr"][0]), (64, 64, QT["Cr"][1])]), (128, [(0, 64, QT["Cr"][2]), (64, 64, QT["Cr"][3])])], tt)
                        elif gi == 8:
                            y, sy, _ = Y.nxt()
                            evac(y[:, 0:192], py[:, 0:192], [spy], [sy], ACT)
                            t1, st1, _ = T1.nxt()
                            ss, sss, _ = SS.nxt()
                            P.op(DVE, I("tensor_tensor", t1[:, 0:128], y[:, 0:128], y[:, 0:128], ALU.mult), reads=[sy], writes=[st1])
                            P.op(DVE, I("tensor_reduce", ss[:, 0:1], t1[:, 0:128], AX.X, ALU.add), reads=[st1], writes=[sss])
                            rstd_ops(ss[:, 0:1], 1, 1.0 / 128, [sss, S_const], [sss])
                            yb, syb, semyb = YB.nxt()
                            P.op(DVE, I("scalar_tensor_tensor", yb[:, 0:128], y[:, 0:128], ss[:, 0:1], g_ckv, op0=ALU.mult, op1=ALU.mult),
                                 reads=[sy, sss, S_lw], writes=[syb])
                            ybr, sybr, _ = YB.nxt()
                            if lat:
                                y3, sy3, _ = Y.nxt()
                                P.op(DVE, I("tensor_copy", y3[:, 0:64], y[:, 128:192]), reads=[sy], writes=[sy3])
                                rope(y3, sy3, 64, c_ropeB, 16, tt, ybr, sybr)
                            else:
                                P.op(DVE, I("tensor_copy", ybr[:, 0:64], y[:, 128:192]), reads=[sy], writes=[sybr])
                            yield
                            to_T(ybr, sybr, [(0, [(0, 64, KT["Cr"][0])])], tt)
                            pt, spt, _ = PTR.nxt()
                            P.op(PE, I("transpose", pt[:, 0, :], yb[:, 0:128], c_identb), reads=[syb, S_const], writes=[spt])
                            ct, sct, _ = CT.nxt()
                            P.op(ACT, I("copy", ct[:, 0, :], pt[:, 0, :]), reads=[spt], writes=[sct])
                            yield
                            p2, sp2, _ = PY2.nxt()
                            fns = [I("matmul", p2[:, c0:c0 + 512], ct[:, 0, :], wukv_b[:, c0:c0 + 512], start=True, stop=True)
                                   for c0 in (0, 512)]
                            P.op(PE, fns, reads=[sct, S_lw], writes=[sp2])
                            y2, sy2, _ = Y2.nxt()
                            evac(y2[:, 0:512], p2[:, 0:512], [sp2], [sy2], ACT)
                            evac(y2[:, 512:1024], p2[:, 512:1024], [sp2], [sy2])
                            y2v = y2.rearrange("p (h d) -> p h d", d=256)
                            ybn, sybn, _ = YB.nxt()
                            P.op(DVE, I("tensor_copy", ybn.rearrange("p (h d) -> p h d", d=128), y2v[:, :, 0:128]), reads=[sy2], writes=[sybn])
                            ybv, sybv, semybv = YB.nxt()
                            P.op(DVE, I("tensor_copy", ybv.rearrange("p (h d) -> p h d", d=128), y2v[:, :, 128:256]), reads=[sy2], writes=[sybv])
                            yield
                            to_T(ybn, sybn, [(h * 128, [(0, 128, KT["Cn"][h])]) for h in range(4)], tt)
                            store_tok(ybv, sybv, semybv, VV["C"][tsl, :])

                    pending = []

                    def advance():
                        for g_ in list(pending):
                            try:
                                next(g_)
                            except StopIteration:
                                pending.remove(g_)

                    def prep_w(gi):
                        g0, gw = GROUPS[gi]
                        wb, swb, semwb = WBF.nxt()
                        ev = {}
                        if hf > 0:
                            ev[0] = [lambda: P.dma(SP, wb[:, :, 0:gw], WB[:, g0:g0 + gw].rearrange("(c p) n -> p c n", p=128), semwb, writes=[swb])]
                            return (wb, swb), ev
                        halves = [(c0, min(256, gw - c0)) for c0 in range(0, gw, 256)]
                        st = {}

                        def load(h):
                            c0, cw = halves[h]
                            wf, swf, semf = WF.nxt()
                            st[h] = (wf, swf)
                            P.dma(SP, wf[:, :, 0:cw], w_in[l, :, g0 + c0:g0 + c0 + cw].rearrange("(c p) n -> p c n", p=128), semf, writes=[swf])

                        def conv(h):
                            c0, cw = halves[h]
                            wf, swf = st[h]
                            P.op(DVE, I("tensor_copy", wb[:, :, c0:c0 + cw], wf[:, :, 0:cw]), reads=[swf], writes=[swb])

                        def store():
                            P.dma(SP, WB[:, g0:g0 + gw].rearrange("(c p) n -> p c n", p=128), wb[:, :, 0:gw], semwb, reads=[swb])

                        ev[0] = [lambda: load(0)]
                        if len(halves) > 1:
                            ev[2] = [lambda: conv(0), lambda: load(1)]
                            ev[4] = [lambda: conv(1), store]
                        else:
                            ev[2] = [lambda: conv(0), store]
                        return (wb, swb), ev

                    glist = [gi for gi in range(len(GROUPS)) if P2_ONLY is None or gi in P2_ONLY]
                    wnext, ev0 = prep_w(glist[0])
                    for k_ in sorted(ev0):
                        for f_ in ev0[k_]:
                            f_()
                    for gpos, gi in enumerate(glist):
                        g0, gw = GROUPS[gi]
                        wb, swb = wnext
                        evn = {}
                        if gpos + 1 < len(glist):
                            wnext, evn = prep_w(glist[gpos + 1])
                        for ti, tt in enumerate(tiles):
                            for f_ in evn.get(ti, []):
                                f_()
                            if hf == 0 and gpos < 8:
                                if ti == 5:
                                    wout_load(gpos)
                                if ti == 7:
                                    wout_conv(gpos)
                            py, spy, _ = PY.nxt()
                            fns = [I("matmul", py[:, 0:gw], hT[:, kc, ti * 128:(ti + 1) * 128], wb[:, kc, 0:gw],
                                     start=(kc == 0), stop=(kc == 15)) for kc in range(16)]
                            P.op(PE, fns, reads=[S_h[ti], swb], writes=[spy])
                            advance()
                            g_new = post(gi, tt, py, spy)
                            pending.append(g_new)
                            try:
                                next(g_new)
                            except StopIteration:
                                pending.remove(g_new)
                    while pending:
                        advance()
                    P.barrier()
                    P.flush()
                    for b_ in (WBF, YB, STG) + ((WF, WOS) if WF is not None else ()):
                        b_.release()
        esr.close()
        if stop_after in ("P1", "P2"):
            break

        with ExitStack() as es:
            KTB = Buf(P, es, nc, "ktb", [128, T], BF16, n=2)
            KRB = Buf(P, es, nc, "krb", [128, T], BF16, n=1)
            VB = Buf(P, es, nc, "vb", [128, NT, 129], BF16, n=2)
            QB = Buf(P, es, nc, "qb", [128, T], BF16, n=4)
            QRB = Buf(P, es, nc, "qrb", [128, T], BF16, n=2)
            ZT = Buf(P, es, nc, "zt", [128, 4, 128], BF16, n=3)
            UT = Buf(P, es, nc, "ut", [128, 4, 128], BF16, n=3, sw=True)
            PTB = Buf(P, es, nc, "ptb", [128, 2, 512], BF16, n=3, dmasem=False)
            PS = Buf(P, es, nc, "ps", [128, 2, 512], F32, n=2, psum=True, dmasem=False)
            PO = Buf(P, es, nc, "po", [128, 4, 256], F32, n=2, psum=True, dmasem=False)
            RC = Buf(P, es, nc, "rc", [128, 8], F32, n=3, dmasem=False)
            OA = Buf(P, es, nc, "oa", [128, 4, 128], F32, n=2, dmasem=False)
            OB = Buf(P, es, nc, "ob", [128, 4, 128], F32, n=2, dmasem=False)
            OC = Buf(P, es, nc, "oc", [128, 4, 128], F32, n=2, dmasem=False)
            for i in range(2):
                P.op(DVE, I("memset", VB.aps[i][:, :, 128:129], 1.0), writes=[VB.slots[i]])
                P.op(DVE, I("memset", QRB.aps[i][64:128, :], 0.0), writes=[QRB.slots[i]])
            P.op(DVE, I("memset", KRB.aps[0][64:128, :], 0.0), writes=[KRB.slots[0]])
            chunks = ([] if last else [(0, 2)]) + [(256 + 512 * c, 4) for c in range(8)]

            def key_units(mixer, q0, nqs):
                if q0 < 256:
                    kbs = [(0, None), (1, None)]
                elif mixer == "A":
                    c = (q0 - 256) // 512
                    kbs = [(0, None), (1, None)]
                    for r in range(-1, 5):
                        lb = 4 * c + r
                        if 0 <= lb < 32:
                            kbs.append((2 + lb, r + 1))
                else:
                    kbs = [(k, None) for k in range(NT)]
                return [kbs[i:i + 2] for i in range(0, len(kbs), 2)]

            work = []

            def add_head(mixer, kparts, qparts, vb, svb, vh, ucol, scale, sink_col, bmode):
                for (q0, nqs) in chunks:
                    units = key_units(mixer, q0, nqs)
                    for ui, u in enumerate(units):
                        work.append(dict(mixer=mixer, parts=kparts_q(kparts, qparts), vb=vb, svb=svb, vh=vh, ucol=ucol,
                                         scale=scale, sink=sink_col, bmode=bmode, q0=q0, nqs=nqs, unit=u,
                                         first=(ui == 0), lastu=(ui == len(units) - 1)))

            def kparts_q(kparts, qparts):
                return list(zip(kparts, qparts))

            def load_k(src_ap, buf):
                ap, s, sem = buf.nxt()
                P.dma(SP, ap[0:src_ap.shape[0], :], src_ap, sem, writes=[s])
                return ap, s

            def load_v(mx, h):
                ap, s, sem = VB.nxt()
                P.dma(SP, ap[:, :, 0:128], VV[mx][:, h * 128:(h + 1) * 128].rearrange("(b p) d -> p b d", p=128), sem, writes=[s])
                return ap, s

            sA = 1.0 / math.sqrt(128.0)
            sB = 1.0 / math.sqrt(64.0)
            sC = 1.0 / math.sqrt(192.0)
            zero_done = [False, False, False]

            def emit_all():
                state = {}

                def qk(w):
                    ps, sps, _ = PS.nxt()
                    nq = w["nqs"] * 128
                    fns = []
                    rd = []
                    for bi, (kb, _m) in enumerate(w["unit"]):
                        np_ = len(w["parts"])
                        for pi, ((kap, sk, K), (qap, sq, _K)) in enumerate(w["parts"]):
                            fns.append(I("matmul", ps[:, bi, 0:nq], kap[0:K, kb * 128:(kb + 1) * 128], qap[0:K, w["q0"]:w["q0"] + nq],
                                         start=(pi == 0), stop=(pi == np_ - 1)))
                            rd += [sk, sq]
                    P.op(PE, fns, reads=rd, writes=[sps])
                    w["ps"], w["sps"] = ps, sps

                def ex(w):
                    pt, spt, _ = PTB.nxt()
                    nq = w["nqs"] * 128
                    nb = len(w["unit"])
                    if nq == 512 and nb == 2:
                        P.op(ACT, I("activation", pt.rearrange("p b q -> p (b q)"), w["ps"].rearrange("p b q -> p (b q)"), AF.Exp, scale=w["scale"]),
                             reads=[w["sps"]], writes=[spt])
                    else:
                        P.op(ACT, [I("activation", pt[:, bi, 0:nq], w["ps"][:, bi, 0:nq], AF.Exp, scale=w["scale"]) for bi in range(nb)],
                             reads=[w["sps"]], writes=[spt])
                    for bi, (kb, m) in enumerate(w["unit"]):
                        if m is not None:
                            P.op(DVE, I("tensor_tensor", pt[:, bi, 0:nq], pt[:, bi, 0:nq], c_amask[:, m, 0:nq], ALU.mult),
                                 reads=[spt, S_const], writes=[spt])
                    w["pt"], w["spt"] = pt, spt

                def pv(w):
                    key = (w["ucol"], w["bmode"], w["q0"])
                    if w["first"]:
                        po, spo, _ = PO.nxt()
                        state[key] = (po, spo)
                    po, spo = state[key]
                    fns = []
                    nb = len(w["unit"])
                    for qs in range(w["nqs"]):
                        for bi, (kb, _m) in enumerate(w["unit"]):
                            fns.append(I("matmul", po[:, qs, 0:129], w["pt"][:, bi, qs * 128:(qs + 1) * 128], w["vb"][:, kb, :],
                                         start=(w["first"] and bi == 0 and qs % 2 == 0), stop=(w["lastu"] and bi == nb - 1),
                                         skip_group_check=True))
                    P.op(PE, fns, reads=[w["spt"], w["svb"]], writes=[spo])
                    if w["lastu"]:
                        finalize(w, po, spo)

                def finalize(w, po, spo):
                    nqs = w["nqs"]
                    q0 = w["q0"]
                    rc, src, _ = RC.nxt()
                    for b0 in range(0, nqs, 2):
                        if w["sink"] is not None:
                            P.op(DVE, I("tensor_scalar", rc[:, b0:b0 + 2], po[:, b0:b0 + 2, 128], esink[:, w["sink"]:w["sink"] + 1], None, op0=ALU.add),
                                 reads=[spo, S_lw], writes=[src])
                        else:
                            P.op(DVE, I("tensor_copy", rc[:, b0:b0 + 2], po[:, b0:b0 + 2, 128]), reads=[spo], writes=[src])
                    P.op(DVE, I("reciprocal", rc[:, 0:nqs], rc[:, 0:nqs]), reads=[src], writes=[src])
                    tok = slice(q0, q0 + nqs * 128)
                    if w["bmode"] == 0:
                        oa, soa, _ = OA.nxt()
                        for qs in range(nqs):
                            P.op(DVE, I("tensor_scalar", oa[:, qs, :], po[:, qs, 0:128], rc[:, qs:qs + 1], None, op0=ALU.mult),
                                 reads=[spo, src], writes=[soa])
                        state[("oa", w["ucol"], q0)] = (oa, soa)
                        return
                    zt, szt, semz = ZT.nxt()
                    P.dma(ACT, zt[:, 0:nqs, :], ZZ[tok, w["ucol"]:w["ucol"] + 128].rearrange("(q p) d -> p q d", p=128), semz, writes=[szt])
                    ut, sut, semu = UT.nxt()
                    if w["bmode"] is None:
                        for qs in range(nqs):
                            P.op(DVE, I("scalar_tensor_tensor", ut[:, qs, :], po[:, qs, 0:128], rc[:, qs:qs + 1], zt[:, qs, :],
                                        op0=ALU.mult, op1=ALU.mult), reads=[spo, src, szt], writes=[sut])
                    else:
                        oa, soa = state.pop(("oa", w["ucol"], q0))
                        ob, sob, _ = OB.nxt()
                        oc, soc, _ = OC.nxt()
                        P.op(DVE, I("tensor_scalar", rc[:, 0:nqs], rc[:, 0:nqs], lamv[:, 0:1], None, op0=ALU.mult), reads=[src, S_lw], writes=[src])
                        for qs in range(nqs):
                            P.op(DVE, I("scalar_tensor_tensor", ob[:, qs, :], po[:, qs, 0:128], rc[:, qs:qs + 1], oa[:, qs, :],
                                        op0=ALU.mult, op1=ALU.add), reads=[spo, src, soa], writes=[sob])
                        P.op(DVE, I("tensor_tensor", oc[:, 0:nqs, :], ob[:, 0:nqs, :], ob[:, 0:nqs, :], ALU.mult), reads=[sob], writes=[soc])
                        P.op(DVE, I("tensor_reduce", rc[:, 4:4 + nqs], oc[:, 0:nqs, :], AX.X, ALU.add), reads=[soc], writes=[src])
                        rstd_ops(rc[:, 4:4 + nqs], nqs, 1.0 / 128, [src, S_const], [src])
                        for qs in range(nqs):
                            P.op(DVE, I("scalar_tensor_tensor", oc[:, qs, :], ob[:, qs, :], rc[:, 4 + qs:5 + qs], g_sub,
                                        op0=ALU.mult, op1=ALU.mult), reads=[sob, src, S_lw], writes=[soc])
                        P.op(DVE, I("tensor_tensor", ut[:, 0:nqs, :], oc[:, 0:nqs, :], zt[:, 0:nqs, :], ALU.mult), reads=[soc, szt], writes=[sut])
                    P.dma(POOL, UU[tok, w["ucol"]:w["ucol"] + 128].rearrange("(q p) d -> p q d", p=128), ut[:, 0:nqs, :], semu, reads=[sut])

                nw = len(work)
                for i in range(nw + 2):
                    if i < nw:
                        qk(work[i])
                        ex(work[i])
                    if i >= 2:
                        pv(work[i - 2])
                work.clear()

            jobs = []

            def mk_gqa(mx, zc, sc_, kv):
                def ld():
                    c = {}
                    c["k"] = load_k(KT[mx][kv], KTB)
                    c["v"] = load_v(mx, kv)
                    c["q"] = [load_k(QT[mx][kv * 2 + g], QB) for g in range(2)]
                    return c

                def wk(c):
                    for g in range(2):
                        hq = kv * 2 + g
                        add_head(mx, [(c["k"][0], c["k"][1], 128)], [(c["q"][g][0], c["q"][g][1], 128)], c["v"][0], c["v"][1], kv,
                                 zc + hq * 128, sc_, hq if mx == "A" else None, None)
                    emit_all()
                return ld, wk

            def mk_mla(h):
                def ld():
                    c = {}
                    if h == 0:
                        kr_state["kr"] = load_k(KT["Cr"][0], KRB)
                    c["k"] = load_k(KT["Cn"][h], KTB)
                    c["v"] = load_v("C", h)
                    c["q"] = load_k(QT["Cn"][h], QB)
                    c["qr"] = load_k(QT["Cr"][h], QRB)
                    return c

                def wk(c):
                    krap, skr = kr_state["kr"]
                    add_head("C", [(c["k"][0], c["k"][1], 128), (krap, skr, 128)], [(c["q"][0], c["q"][1], 128), (c["qr"][0], c["qr"][1], 128)],
                             c["v"][0], c["v"][1], h, 1024 + h * 128, sC, None, None)
                    emit_all()
                return ld, wk

            def mk_diff(h):
                def ld():
                    c = {}
                    c["k"] = load_k(KT["B"][h], KTB)
                    c["v"] = load_v("B", h)
                    qs_ = []
                    for cc in range(2):
                        qap, sq, semq = QB.nxt()
                        z0, z1 = (64, 128) if cc == 0 else (0, 64)
                        d0, d1 = (0, 64) if cc == 0 else (64, 128)
                        P.op(DVE if cc == 0 else POOL, I("memset", qap[z0:z1, :], 0.0), writes=[sq])
                        P.dma(SP, qap[d0:d1, :], QT["B"][h][d0:d1, :], semq, writes=[sq])
                        qs_.append((qap, sq))
                    c["q"] = qs_
                    return c

                def wk(c):
                    for cc in range(2):
                        add_head("B", [(c["k"][0], c["k"][1], 128)], [(c["q"][cc][0], c["q"][cc][1], 128)], c["v"][0], c["v"][1], h,
                                 512 + h * 128, sB, None, cc)
                    work.sort(key=lambda w: (w["q0"], w["bmode"]))
                    emit_all()
                return ld, wk

            kr_state = {}
            for mx, zc, sc_ in (("A", 0, sA), ("D", 1536, sA)):
                for kv in range(2):
                    jobs.append(mk_gqa(mx, zc, sc_, kv))
            for h in range(4):
                jobs.append(mk_mla(h))
            for h in range(4):
                jobs.append(mk_diff(h))
            ctxs = [None] * len(jobs)
            ctxs[0] = jobs[0][0]()
            for ji in range(len(jobs)):
                if ji + 1 < len(jobs):
                    ctxs[ji + 1] = jobs[ji + 1][0]()
                jobs[ji][1](ctxs[ji])
            P.barrier()
            P.flush()
            for b_ in (KTB, KRB, VB, QB, QRB, ZT, UT):
                b_.release()
        if stop_after == "P3":
            break

        with ExitStack() as es:
            wo = sb(es, "wo", [128, 16, D], BF16)
            S_wo = Slot()
            c_fng = sb(es, "c_fng", [128, D], F32)
            fsem = P.getsem()
            P.dma(SP, c_fng, fng.partition_broadcast(128), fsem, writes=[S_const])
            UB = Buf(P, es, nc, "ub", [128, D], BF16, n=3)
            XB = Buf(P, es, nc, "xb4", [128, D], F32, n=3)
            UTT = Buf(P, es, nc, "utt", [128, 16, 128], BF16, n=2, dmasem=False)
            XN = Buf(P, es, nc, "xn", [128, D], F32, n=2, sw=True)
            JK = Buf(P, es, nc, "jk4", [128, D], BF16, n=1, dmasem=False)
            ST = Buf(P, es, nc, "st4", [128, 1], F32, n=2, dmasem=False)
            PTR = Buf(P, es, nc, "ptr4", [128, 8, 128], BF16, n=2, psum=True, dmasem=False)
            PY = Buf(P, es, nc, "py4", [128, 512], F32, n=4, psum=True, dmasem=False)
            wsem = P.getsem()
            for i in range(4):
                P.dma(SP if i % 2 == 0 else ACT, wo[:, :, i * 512:(i + 1) * 512],
                      WOB[:, i * 512:(i + 1) * 512].rearrange("(c p) n -> p c n", p=128), wsem, writes=[S_wo])
            def p4a(tt):
                tsl = slice(tt * 128, (tt + 1) * 128)
                ub, sub_, semu = UB.nxt()
                P.dma(SP, ub, UU[tsl, :], semu, writes=[sub_])
                xt, sx, semx = XB.nxt()
                P.dma(ACT, xt, xsrc[tsl, :], semx, writes=[sx])
                utt, sutt, _ = UTT.nxt()
                for h8 in range(2):
                    pt, spt, _ = PTR.nxt()
                    fns = [I("transpose", pt[:, j, :], ub[:, (h8 * 8 + j) * 128:(h8 * 8 + j + 1) * 128], c_identb) for j in range(8)]
                    P.op(PE, fns, reads=[sub_, S_const], writes=[spt])
                    if h8 == 0:
                        P.op(ACT, I("copy", utt[:, 0:8, :], pt), reads=[spt], writes=[sutt])
                    else:
                        P.op(DVE, I("tensor_copy", utt[:, 8:16, :], pt), reads=[spt], writes=[sutt])
                return utt, sutt, xt, sx

            def p4b(tt, utt, sutt, xt, sx):
                v = 1 if tt < NC_T else 0
                tsl = slice(tt * 128, (tt + 1) * 128)
                xn, sxn, semn = XN.nxt()
                for cq in range(4):
                    py, spy, _ = PY.nxt()
                    fns = [I("matmul", py, utt[:, mc, :], wo[:, mc, cq * 512:(cq + 1) * 512], start=(mc == 0), stop=(mc == 15))
                           for mc in range(16)]
                    P.op(PE, fns, reads=[sutt, S_wo], writes=[spy])
                    cs = slice(cq * 512, (cq + 1) * 512)
                    P.op(DVE, I("tensor_tensor", xn[:, cs], py, gate_bc[:, v, cs], ALU.mult), reads=[spy, S_mod], writes=[sxn])
                P.op(POOL, I("tensor_tensor", xn, xn, xt, ALU.add), reads=[sxn, sx], writes=[sxn])
                if not last:
                    P.dma(POOL, XS[tsl, :], xn, semn, reads=[sxn])
                else:
                    jk, sj, _ = JK.nxt()
                    st, sst, _ = ST.nxt()
                    P.op(ACT, I("activation", jk, xn, AF.Square, scale=1.0 / math.sqrt(D), accum_out=st), reads=[sxn], writes=[sj, sst])
                    rstd_ops(st, 1, 1.0, [sst, S_const], [sst])
                    P.op(DVE, I("scalar_tensor_tensor", xn, xn, st, c_fng, op0=ALU.mult, op1=ALU.mult), reads=[sxn, sst, S_const], writes=[sxn])
                    P.dma(POOL, out[(tt - NC_T) * 128:(tt - NC_T + 1) * 128, :], xn, semn, reads=[sxn])

            tl4 = list(range(NC_T if last else 0, NT))
            prev4 = None
            for i4, tt in enumerate(tl4):
                cur4 = p4a(tt)
                if prev4 is not None:
                    p4b(tl4[i4 - 1], *prev4)
                prev4 = cur4
            p4b(tl4[-1], *prev4)
            P.barrier()
            P.flush()
            for b_ in (UB, XB, XN):
                b_.release()
            P.pool.append(fsem)
            P.pool.append(wsem)

    P.barrier()
    P.flush()
    P.close()
    top.close()
    return nc, dbg


def _rope_tables(rot_dim):
    rows = 4096 // 64
    row = np.broadcast_to(np.arange(rows)[:, None], (rows, 64)).reshape(-1).astype(np.float32)
    col = np.broadcast_to(np.arange(64)[None, :], (rows, 64)).reshape(-1).astype(np.float32)
    axis_dim = rot_dim // 2
    inv_freq = (np.float32(10000.0) ** (-np.arange(0, axis_dim, 2, dtype=np.float32) / np.float32(axis_dim))).astype(np.float32)
    ang_r = (row[:, None] * inv_freq[None, :]).astype(np.float32)
    ang_c = (col[:, None] * inv_freq[None, :]).astype(np.float32)
    ang = np.concatenate([ang_r, ang_r, ang_c, ang_c], axis=-1)
    cos = np.cos(ang).astype(np.float32)
    sin = np.sin(ang).astype(np.float32)
    q = rot_dim // 4
    sgn = np.concatenate([-np.ones(q), np.ones(q), -np.ones(q), np.ones(q)]).astype(np.float32)
    return cos, sin * sgn[None, :]


def _consts():
    cH, sH = _rope_tables(128)
    cB, sB = _rope_tables(64)
    ropeH = np.stack([cH, sH]).astype(np.float32)
    ropeB = np.stack([cB, sB]).astype(np.float32)
    identf = np.eye(128, dtype=np.float32)
    identb = np.eye(128, dtype=np.float32).astype(ml_dtypes.bfloat16)
    sel = np.zeros((2, 2, 128), np.float32)
    sel[0, 0, :] = 1.0
    sel[1, 1, :] = 1.0
    am = np.zeros((6, 128, 512), np.float32)
    i = np.arange(128)[:, None]
    qq = np.arange(512)[None, :]
    for m in range(6):
        kpos = (m - 1) * 128 + i
        am[m] = (np.abs(qq - kpos) <= 128)
    return dict(ropeH=ropeH, ropeB=ropeB, identf=identf, identb=identb, sel=sel, amask=am.astype(ml_dtypes.bfloat16))


_CACHE = {}


def _prep_inputs(inp, b, consts):
    f = lambda a: np.ascontiguousarray(np.asarray(a, dtype=np.float32))
    m = dict(consts)
    m["xin"] = np.ascontiguousarray(np.concatenate([inp["ctx"][b], inp["x"][b]], axis=0).astype(np.float32))
    cv = np.stack([inp["c"][b], inp["c_ctx"]], axis=0).astype(np.float32)
    m["cT"] = np.ascontiguousarray(cv.reshape(2, 16, 128).transpose(2, 1, 0))
    m["w_mod"] = f(inp["w_mod"])
    bm = np.asarray(inp["b_mod"], np.float32)
    m["bmod_row"] = np.ascontiguousarray(bm.reshape(DEPTH, 1, 3 * D))
    m["normg_col"] = np.ascontiguousarray(np.asarray(inp["norm_g"], np.float32).reshape(DEPTH, 16, 128).transpose(0, 2, 1))
    m["w_in"] = f(inp["w_in"])
    m["cqg"] = f(inp["c_q_norm_g"]).reshape(DEPTH, 1, 448)
    m["ckvg"] = f(inp["c_kv_norm_g"]).reshape(DEPTH, 1, 128)
    m["w_uq"] = f(inp["c_w_uq"])
    m["w_ukv"] = f(inp["c_w_ukv"])
    m["dqg"] = f(inp["d_q_norm_g"]).reshape(DEPTH, 1, 128)
    m["dkg"] = f(inp["d_k_norm_g"]).reshape(DEPTH, 1, 128)
    m["a_sink"] = f(inp["a_sink"]).reshape(DEPTH, 1, 4)
    m["b_lambda"] = f(inp["b_lambda"]).reshape(DEPTH, 1, 256)
    m["sublng"] = f(inp["b_subln_g"]).reshape(DEPTH, 1, 128)
    m["w_out"] = f(inp["w_out"])
    m["fng"] = f(inp["final_norm_g"]).reshape(1, D)
    return m


def kernel(**inputs):
    inp = {k: np.asarray(v) for k, v in inputs.items()}
    if "nc" not in _CACHE:
        _CACHE["nc"] = build_program()[0]
    nc = _CACHE["nc"]
    consts = _consts()
    in_maps = [_prep_inputs(inp, b, consts) for b in range(8)]
    res = run_bass_kernel_spmd(nc, in_maps, core_ids=list(range(8)))
    return np.stack([np.asarray(r["out"], dtype=np.float32) for r in res.results], axis=0)
```
